# Optimizing a Trainium2 kernel written in Bass

```python
import jax, jax.numpy as jnp
from jax import lax
import numpy as np

D_MODEL = 2048
BATCH = 8
SEQ = 2048
DEPTH = 1

MEM_LEN = 256
HEAD_DIM = 128
MOBA_HEADS = 8
MOBA_BLOCK = 256
MOBA_TOPK = 3
MOBA_Q_CHUNK = 16
DIL_CONFIGS = ((128, 1), (512, 4), (2048, 16))
DIL_HEADS_PER_GROUP = 4
MEM_HEADS = 4
MEM_HEAD_DIM = 256
ROPE_THETA = 10000.0
RMS_EPS = 1e-6
NEG = -1e30
N_BRANCH = 3

W_A = MOBA_HEADS * HEAD_DIM
W_B_QKV = len(DIL_CONFIGS) * DIL_HEADS_PER_GROUP * HEAD_DIM
W_B_OUT = DIL_HEADS_PER_GROUP * HEAD_DIM
W_M = MEM_HEADS * MEM_HEAD_DIM
IN_SPLITS = (W_A, W_A, W_A, W_A, W_B_QKV, W_B_QKV, W_B_QKV, W_B_OUT, W_M, W_M, N_BRANCH * D_MODEL)
IN_WIDTH = sum(IN_SPLITS)

kernel_name = "hybrid_moba_dilated_memory_gated"


def rms_norm(x, g):
    xf = x.astype(jnp.float32)
    y = xf * lax.rsqrt(jnp.mean(xf * xf, axis=-1, keepdims=True) + RMS_EPS)
    return (y * g.astype(jnp.float32)).astype(x.dtype)


def rope(x):
    S, D = x.shape[1], x.shape[-1]
    half = D // 2
    inv = ROPE_THETA ** (-jnp.arange(half, dtype=jnp.float32) / half)
    ang = jnp.arange(S, dtype=jnp.float32)[:, None] * inv[None, :]
    cos = jnp.cos(ang)[None, :, None, :]
    sin = jnp.sin(ang)[None, :, None, :]
    xf = x.astype(jnp.float32)
    x1, x2 = xf[..., :half], xf[..., half:]
    return jnp.concatenate([x1 * cos - x2 * sin, x2 * cos + x1 * sin], axis=-1).astype(x.dtype)


def moba_attention(q, k, v):
    B, S, H, D = q.shape
    C = MOBA_Q_CHUNK
    nb = -(-S // MOBA_BLOCK)
    s_pad = nb * MOBA_BLOCK
    scale = D ** -0.5
    qh = q.transpose(0, 2, 1, 3)
    pad = ((0, 0), (0, 0), (0, s_pad - S), (0, 0))
    kb = jnp.pad(k.transpose(0, 2, 1, 3), pad).reshape(B, H, nb, MOBA_BLOCK, D)
    vb = jnp.pad(v.transpose(0, 2, 1, 3), pad).reshape(B, H, nb, MOBA_BLOCK, D)
    k_mean = jnp.mean(kb.astype(jnp.float32), axis=3)
    gate = jnp.einsum('bhsd,bhnd->bhsn', qh.astype(jnp.float32), k_mean)
    q_blk = jnp.arange(S) // MOBA_BLOCK
    past = jnp.arange(nb)[None, :] < q_blk[:, None]
    gate = jnp.where(past[None, None], gate, -jnp.inf)
    n_sel = max(1, min(MOBA_TOPK, nb - 1))
    _, sel = lax.top_k(gate, n_sel)
    sel_valid = jnp.arange(n_sel)[None, :] < q_blk[:, None]

    nq = S // C
    qc = qh.reshape(B, H, nq, C, D).transpose(2, 0, 1, 3, 4)
    selc = sel.reshape(B, H, nq, C, n_sel).transpose(2, 0, 1, 3, 4)
    validc = sel_valid.reshape(nq, C, n_sel)
    bi = jnp.arange(B)[:, None, None, None]
    hi = jnp.arange(H)[None, :, None, None]

    def chunk(args):
        c, q_c, sel_c, valid_c = args
        start = c * C
        own = start // MOBA_BLOCK
        k_own = lax.dynamic_index_in_dim(kb, own, axis=2, keepdims=False)
        v_own = lax.dynamic_index_in_dim(vb, own, axis=2, keepdims=False)
        k_sel = kb[bi, hi, sel_c]
        v_sel = vb[bi, hi, sel_c]
        s_sel = jnp.einsum('bhqd,bhqnkd->bhqnk', q_c, k_sel,
                           preferred_element_type=jnp.float32) * scale
        s_sel = jnp.where(valid_c[None, None, :, :, None], s_sel, NEG)
        s_own = jnp.einsum('bhqd,bhkd->bhqk', q_c, k_own,
                           preferred_element_type=jnp.float32) * scale
        q_pos = start + jnp.arange(C)
        k_pos = own * MOBA_BLOCK + jnp.arange(MOBA_BLOCK)
        s_own = jnp.where(k_pos[None, :] <= q_pos[:, None], s_own, NEG)
        s_all = jnp.concatenate([s_sel.reshape(B, H, C, n_sel * MOBA_BLOCK), s_own], axis=-1)
        p = jax.nn.softmax(s_all, axis=-1)
        p_sel = p[..., :n_sel * MOBA_BLOCK].reshape(B, H, C, n_sel, MOBA_BLOCK).astype(v.dtype)
        p_own = p[..., n_sel * MOBA_BLOCK:].astype(v.dtype)
        return (jnp.einsum('bhqnk,bhqnkd->bhqd', p_sel, v_sel)
                + jnp.einsum('bhqk,bhkd->bhqd', p_own, v_own))

    out = lax.map(chunk, (jnp.arange(nq), qc, selc, validc))
    return out.transpose(1, 0, 3, 2, 4).reshape(B, S, H, D)


def dilated_group(q, k, v, window, dilation):
    B, S, H, D = q.shape
    band = window // dilation
    L = S // dilation
    nblk = -(-L // band)
    Lp = nblk * band
    scale = D ** -0.5

    def to_sub(t):
        t = t.reshape(B, L, dilation, H, D).transpose(0, 2, 3, 1, 4)
        t = jnp.pad(t, ((0, 0), (0, 0), (0, 0), (0, Lp - L), (0, 0)))
        return t.reshape(B, dilation, H, nblk, band, D)

    def with_prev(t):
        prev = jnp.concatenate([jnp.zeros_like(t[:, :, :, :1]), t[:, :, :, :-1]], axis=3)
        return jnp.concatenate([prev, t], axis=4)

    qs, ks, vs = to_sub(q), with_prev(to_sub(k)), with_prev(to_sub(v))
    s = jnp.einsum('brhnqd,brhnkd->brhnqk', qs, ks, preferred_element_type=jnp.float32) * scale
    n_i = jnp.arange(nblk)[:, None, None]
    q_i = jnp.arange(band)[None, :, None]
    k_i = jnp.arange(2 * band)[None, None, :] - band
    delta = q_i - k_i
    mask = (delta >= 0) & (delta <= band) & ((n_i > 0) | (k_i >= 0))
    s = jnp.where(mask, s, NEG)
    m = jnp.max(s, axis=-1, keepdims=True)
    p = jnp.exp(s - m)
    l = jnp.sum(p, axis=-1, keepdims=True)
    o = jnp.einsum('brhnqk,brhnkd->brhnqd', p, vs.astype(jnp.float32)) / l
    lse = (m + jnp.log(l))[..., 0]
    o = o.reshape(B, dilation, H, Lp, D)[:, :, :, :L].transpose(0, 3, 1, 2, 4).reshape(B, S, H, D)
    lse = lse.reshape(B, dilation, H, Lp)[:, :, :, :L].transpose(0, 3, 1, 2).reshape(B, S, H)
    return o, lse


def dilated_mixture(q, k, v):
    B, S, _, D = q.shape
    G = len(DIL_CONFIGS)
    qg = q.reshape(B, S, G, DIL_HEADS_PER_GROUP, D)
    kg = k.reshape(B, S, G, DIL_HEADS_PER_GROUP, D)
    vg = v.reshape(B, S, G, DIL_HEADS_PER_GROUP, D)
    outs, lses = [], []
    for g, (window, dilation) in enumerate(DIL_CONFIGS):
        o, lse = dilated_group(qg[:, :, g], kg[:, :, g], vg[:, :, g], window, dilation)
        outs.append(o)
        lses.append(lse)
    w = jax.nn.softmax(jnp.stack(lses, axis=0), axis=0)
    out = jnp.sum(w[..., None] * jnp.stack(outs, axis=0), axis=0)
    return out.astype(v.dtype)


def memory_attention(q, mk, mv):
    scale = q.shape[-1] ** -0.5
    s = jnp.einsum('bshd,bmhd->bhsm', q, mk, preferred_element_type=jnp.float32) * scale
    p = jax.nn.softmax(s, axis=-1).astype(mv.dtype)
    return jnp.einsum('bhsm,bmhd->bshd', p, mv)


def setup_inputs(seed: int = 0) -> dict:
    key = jax.random.key(seed)
    ks = jax.random.split(key, 11)
    f32 = jnp.float32
    x = jax.random.normal(ks[0], (BATCH, SEQ, D_MODEL), f32)
    mem = jax.random.normal(ks[1], (BATCH, MEM_LEN, D_MODEL), f32)
    norm_in_g = 1.0 + 0.05 * jax.random.normal(ks[2], (DEPTH, D_MODEL), f32)
    norm_mem_g = 1.0 + 0.05 * jax.random.normal(ks[3], (D_MODEL,), f32)
    w_in = jax.random.normal(ks[4], (DEPTH, D_MODEL, IN_WIDTH), f32) * D_MODEL ** -0.5
    w_mem_kv = jax.random.normal(ks[5], (DEPTH, D_MODEL, 2 * W_M), f32) * D_MODEL ** -0.5
    w_proj_a = jax.random.normal(ks[6], (DEPTH, W_A, D_MODEL), f32) * W_A ** -0.5
    w_proj_b = jax.random.normal(ks[7], (DEPTH, W_B_OUT, D_MODEL), f32) * W_B_OUT ** -0.5
    w_proj_m = jax.random.normal(ks[8], (DEPTH, W_M, D_MODEL), f32) * W_M ** -0.5
    w_out = jax.random.normal(ks[9], (DEPTH, D_MODEL, D_MODEL), f32) * D_MODEL ** -0.5
    norm_final_g = 1.0 + 0.05 * jax.random.normal(ks[10], (D_MODEL,), f32)
    return {"x": x, "mem": mem, "norm_in_g": norm_in_g, "norm_mem_g": norm_mem_g,
            "w_in": w_in, "w_mem_kv": w_mem_kv, "w_proj_a": w_proj_a, "w_proj_b": w_proj_b,
            "w_proj_m": w_proj_m, "w_out": w_out, "norm_final_g": norm_final_g}


def reference(x, mem, norm_in_g, norm_mem_g, w_in, w_mem_kv, w_proj_a, w_proj_b, w_proj_m,
              w_out, norm_final_g):
    B, S, _ = x.shape
    M = mem.shape[1]
    split_points = [int(p) for p in np.cumsum(IN_SPLITS)[:-1]]
    mem_n = rms_norm(mem, norm_mem_g)
    for layer in range(DEPTH):
        h = rms_norm(x, norm_in_g[layer])
        proj = jnp.einsum('bsd,de->bse', h, w_in[layer])
        (qa, ka, va, za, qb, kb, vb, zb, qm, zm, gates) = jnp.split(proj, split_points, axis=-1)

        qa = rope(qa.reshape(B, S, MOBA_HEADS, HEAD_DIM))
        ka = rope(ka.reshape(B, S, MOBA_HEADS, HEAD_DIM))
        va = va.reshape(B, S, MOBA_HEADS, HEAD_DIM)
        ya = moba_attention(qa, ka, va).reshape(B, S, W_A) * jax.nn.silu(za)

        hb = len(DIL_CONFIGS) * DIL_HEADS_PER_GROUP
        qb = rope(qb.reshape(B, S, hb, HEAD_DIM))
        kb = rope(kb.reshape(B, S, hb, HEAD_DIM))
        vb = vb.reshape(B, S, hb, HEAD_DIM)
        yb = dilated_mixture(qb, kb, vb).reshape(B, S, W_B_OUT) * jax.nn.silu(zb)

        mkv = jnp.einsum('bmd,de->bme', mem_n, w_mem_kv[layer])
        mk, mv = jnp.split(mkv, 2, axis=-1)
        ym = memory_attention(qm.reshape(B, S, MEM_HEADS, MEM_HEAD_DIM),
                              mk.reshape(B, M, MEM_HEADS, MEM_HEAD_DIM),
                              mv.reshape(B, M, MEM_HEADS, MEM_HEAD_DIM)).reshape(B, S, W_M)
        ym = ym * jax.nn.silu(zm)

        g_a, g_b, g_m = jnp.split(jax.nn.sigmoid(gates), N_BRANCH, axis=-1)
        merged = (g_a * jnp.einsum('bse,ed->bsd', ya, w_proj_a[layer])
                  + g_b * jnp.einsum('bse,ed->bsd', yb, w_proj_b[layer])
                  + g_m * jnp.einsum('bse,ed->bsd', ym, w_proj_m[layer]))
        x = x + jnp.einsum('bsd,de->bse', merged, w_out[layer])
    return rms_norm(x, norm_final_g)
```

```python
import numpy as np
from contextlib import ExitStack
import concourse.bass as bass
import concourse.mybir as mybir
from concourse.bass_utils import run_bass_kernel_spmd

F32 = mybir.dt.float32
BF16 = mybir.dt.bfloat16
AF = mybir.ActivationFunctionType
ALU = mybir.AluOpType
AX = mybir.AxisListType

ENG = ("pe", "act", "dve", "pool", "sp")
BLK = {"pe": "tensor", "act": "scalar", "dve": "vector", "pool": "gpsimd", "sp": "sync"}
DMAQ = ("sp", "pool")
NR = 12
SAME_ENGINE_SYNC = True
NEG = -30000.0
D = 2048
S_LEN = 2048
EPS = 1e-6


class Op:
    __slots__ = ("eng", "fn", "deps", "sig", "ev", "dma", "ph")


class Buf:
    __slots__ = ("lw", "rd")

    def __init__(self):
        self.lw = None
        self.rd = []


class Sched:
    def __init__(self, nc, es):
        self.nc = nc
        self.esem = {e: es.enter_context(nc.semaphore("s_" + e)) for e in ENG}
        self.ecnt = {e: 0 for e in ENG}
        self.dq = {q: [es.enter_context(nc.semaphore("d_%s%d" % (q, i))) for i in range(NR)] for q in DMAQ}
        self.dn = {q: 0 for q in DMAQ}
        self.dlast = {q: [None] * NR for q in DMAQ}
        self.phase_sem = es.enter_context(nc.semaphore("phase"))
        self.phase_n = 0
        self.waited = {e: {} for e in ENG}
        self.ops = {e: [] for e in ENG}

    def add(self, eng, fn, r=(), w=(), dma=False):
        op = Op()
        op.eng = eng
        op.fn = fn
        op.dma = dma
        op.sig = False
        op.ev = None
        op.ph = self.phase_n
        deps = []
        for b in r:
            if b.lw is not None:
                deps.append(b.lw)
        for b in w:
            if b.lw is not None:
                deps.append(b.lw)
            deps.extend(b.rd)
        if dma:
            n = self.dn[eng]
            slot = n % NR
            prev = self.dlast[eng][slot]
            if prev is not None:
                deps.append(prev)
            self.dlast[eng][slot] = op
            self.dn[eng] = n + 1
            op.ev = (self.dq[eng][slot], 16 * (n // NR + 1))
            op.sig = True
        out = []
        seen = set()
        for d in deps:
            if id(d) in seen or d is op or d.ph != self.phase_n:
                continue
            seen.add(id(d))
            if (not d.dma) and (not dma) and d.eng == eng:
                if eng == "pe" or not SAME_ENGINE_SYNC:
                    continue
            d.sig = True
            out.append(d)
        op.deps = out
        for b in r:
            if dma:
                b.rd.append(op)
            else:
                b.rd = [o for o in b.rd if o.dma or o.eng != eng] + [op]
        for b in w:
            b.lw = op
            b.rd = []
        self.ops[eng].append(op)
        return op

    def _emit_engine(self, e, eng):
        wt = self.waited[e]
        if self.phase_n > 0:
            eng.wait_ge(self.phase_sem, len(ENG) * self.phase_n)
        for op in self.ops[e]:
            need = {}
            for d in op.deps:
                sem, val = d.ev
                k = id(sem)
                if wt.get(k, 0) < val and need.get(k, (None, 0))[1] < val:
                    need[k] = (sem, val)
            for k, (sem, val) in need.items():
                eng.wait_ge(sem, val)
                wt[k] = val
            ins = op.fn(eng)
            if op.sig:
                ins.then_inc(op.ev[0], 16 if op.dma else 1)
        if e in DMAQ:
            for o in self.dlast[e]:
                if o is not None:
                    sem, val = o.ev
                    if wt.get(id(sem), 0) < val:
                        eng.wait_ge(sem, val)
                        wt[id(sem)] = val
        eng.drain().then_inc(self.phase_sem, 1)

    def emit(self):
        for e in ENG:
            c = self.ecnt[e]
            for op in self.ops[e]:
                if (not op.dma) and op.sig:
                    c += 1
                    op.ev = (self.esem[e], c)
            self.ecnt[e] = c
        with self.nc.Block() as block:
            for e in ENG:
                getattr(block, BLK[e])(lambda eng, e=e: self._emit_engine(e, eng))
        self.phase_n += 1
        self.ops = {e: [] for e in ENG}

    def mm(self, out, lhsT, rhs, start, stop, r=(), w=(), skip=False):
        return self.add("pe", lambda e: e.matmul(out, lhsT, rhs, start=start, stop=stop, skip_group_check=skip), r, w)

    def tr(self, out, in_, ident, r=(), w=()):
        return self.add("pe", lambda e: e.transpose(out, in_, ident), r, w)

    def act(self, out, in_, func, r=(), w=(), scale=None, accum_out=None, bias=None):
        kw = {}
        if scale is not None:
            kw["scale"] = scale
        if accum_out is not None:
            kw["accum_out"] = accum_out
        if bias is not None:
            kw["bias"] = bias
        return self.add("act", lambda e: e.activation(out=out, in_=in_, func=func, **kw), r, w)

    def tt(self, eng, out, in0, in1, op, r=(), w=()):
        return self.add(eng, lambda e: e.tensor_tensor(out=out, in0=in0, in1=in1, op=op), r, w)

    def ts(self, eng, out, in0, s1, s2, op0, op1, r=(), w=()):
        if op1 is None:
            return self.add(eng, lambda e: e.tensor_scalar(out=out, in0=in0, scalar1=s1, scalar2=None, op0=op0), r, w)
        return self.add(eng, lambda e: e.tensor_scalar(out=out, in0=in0, scalar1=s1, scalar2=s2, op0=op0, op1=op1), r, w)

    def stt(self, out, in0, scalar, in1, op0, op1, r=(), w=()):
        return self.add("dve", lambda e: e.scalar_tensor_tensor(out=out, in0=in0, scalar=scalar, in1=in1, op0=op0, op1=op1), r, w)

    def copy(self, eng, out, in_, r=(), w=()):
        if eng == "act":
            return self.add("act", lambda e: e.copy(out=out, in_=in_), r, w)
        return self.add(eng, lambda e: e.tensor_copy(out=out, in_=in_), r, w)

    def dma(self, q, out, in_, r=(), w=()):
        return self.add(q, lambda e: e.dma_start(out=out, in_=in_), r, w, dma=True)


class WStream:
    def __init__(self, S, ring, srcs, depth):
        self.S, self.ring, self.srcs, self.depth = S, ring, list(srcs), depth
        self.slots = []
        self.i = 0

    def prime(self):
        while len(self.slots) < min(len(self.srcs), self.i + self.depth):
            t, b = self.ring.next()
            self.S.dma("pool", t[:], self.srcs[len(self.slots)], w=[b])
            self.slots.append((t, b))

    def get(self):
        while len(self.slots) < min(len(self.srcs), self.i + self.depth):
            t, b = self.ring.next()
            self.S.dma("pool", t[:], self.srcs[len(self.slots)], w=[b])
            self.slots.append((t, b))
        r = self.slots[self.i]
        self.i += 1
        return r


class Ring:
    def __init__(self, items):
        self.items = [(t, Buf()) for t in items]
        self.i = 0

    def next(self):
        it = self.items[self.i % len(self.items)]
        self.i += 1
        return it


def build_nc(debug=False):
    nc = bass.Bass("TRN2", target_bir_lowering=False)
    dt = nc.dram_tensor
    x = dt("x", [2048, 2048], F32, kind="ExternalInput").ap()
    mem = dt("mem", [256, 2048], F32, kind="ExternalInput").ap()
    w_in = dt("w_in", [136, 128, 2048], F32, kind="ExternalInput").ap()
    w_kv = dt("w_kv", [16, 128, 2048], F32, kind="ExternalInput").ap()
    w_pab = dt("w_pab", [16, 128, 1536], F32, kind="ExternalInput").ap()
    w_pm = dt("w_pm", [16, 128, 1024], F32, kind="ExternalInput").ap()
    w_out = dt("w_out", [16, 128, 2048], F32, kind="ExternalInput").ap()
    g_in = dt("g_in", [128, 2048], F32, kind="ExternalInput").ap()
    g_mem = dt("g_mem", [128, 2048], F32, kind="ExternalInput").ap()
    g_fin = dt("g_fin", [128, 2048], F32, kind="ExternalInput").ap()
    cos_d = dt("cosT", [128, 2048], F32, kind="ExternalInput").ap()
    sin_d = dt("sinX", [128, 2048], F32, kind="ExternalInput").ap()
    cmask_d = dt("cmask", [128, 2048], F32, kind="ExternalInput").ap()
    bmask_d = dt("bmask", [128, 1280], F32, kind="ExternalInput").ap()
    ident_d = dt("ident", [128, 128], F32, kind="ExternalInput").ap()
    negsel_d = dt("negsel", [128, 1024], F32, kind="ExternalInput").ap()
    pastb_d = dt("pastb", [128, 64], F32, kind="ExternalInput").ap()
    pastm_d = dt("pastm", [128, 64], F32, kind="ExternalInput").ap()
    out = dt("out", [2048, 2048], F32, kind="ExternalOutput").ap()
    skind = "ExternalOutput" if debug else "Internal"
    sg = dt("sg_scr", [16, 4, 128, 3, 512], BF16, kind=skind).ap()
    yT = dt("yT_scr", [20, 128, 2048], BF16, kind=skind).ap()
    wo_scr = dt("wo_scr", [16, 128, 2048], BF16, kind="Internal").ap()

    with ExitStack() as es:
        S = Sched(nc, es)
        sb = lambda st, name, shape, dtype: st.enter_context(nc.sbuf_tensor("sb_" + name, shape, dtype))
        ps = lambda st, name, shape, dtype: st.enter_context(nc.psum_tensor("ps_" + name, shape, dtype))
        ident = sb(es, "ident", [128, 128], BF16)
        ones = sb(es, "ones", [128, 128], BF16)
        identb = Buf()

        es_h = ExitStack()
        hT = sb(es_h, "hT", [128, 16, 2048], BF16)
        memT = sb(es_h, "memT", [128, 16, 256], BF16)

        def proj_block(wtile, wbuf, rhs_fn, nqb, N, ring, epi, hook=None):
            for qb in range(nqb):
                pt_, pb_ = ring.next()
                for kc in range(16):
                    S.mm(pt_[:, 0:N], wtile[:, kc * 128:(kc + 1) * 128], rhs_fn(kc, qb), kc == 0, kc == 15, r=[wbuf], w=[pb_])
                epi(qb, pt_, pb_)
                if qb == 0 and hook is not None:
                    hook()

        h_rhs = lambda kc, qb: hT[:, kc, qb * 512:(qb + 1) * 512]

        with ExitStack() as pe_:
            xt = Ring([sb(pe_, "xt%d" % i, [128, 2048], F32) for i in range(4)])
            hb = Ring([sb(pe_, "hb%d" % i, [128, 2048], BF16) for i in range(3)])
            junk = sb(pe_, "junk", [128, 2048], BF16)
            grep = sb(pe_, "grep", [128, 2048], F32)
            gmrep = sb(pe_, "gmrep", [128, 2048], F32)
            st = sb(pe_, "st", [128, 4 * 18], F32)
            tp = Ring([ps(pe_, "tp%d" % i, [128, 512], BF16) for i in range(4)])
            wr = Ring([sb(pe_, "wr%d" % i, [128, 2048], BF16) for i in range(8)])
            sgt = Ring([sb(pe_, "sgt%d" % i, [128, 2048], BF16) for i in range(8)])
            pr = Ring([ps(pe_, "pr%d" % i, [128, 512], F32) for i in range(4)])
            gb, gmb, junkb = Buf(), Buf(), Buf()
            hTb = {(t_, g_): Buf() for t_ in range(16) for g_ in range(4)}
            S.dma("pool", ident[:], ident_d, w=[identb])
            S.add("dve", lambda e: e.memset(ones[:], 1.0))
            S.dma("sp", grep[:], g_in, w=[gb])
            S.dma("sp", gmrep[:], g_mem, w=[gmb])
            ws1 = WStream(S, wr, [w_in[88 + gi] for gi in range(48)], 2)
            cpc = [0]
            stt_ = {}

            def stA(tt_):
                is_mem = tt_ >= 16
                src = mem[(tt_ - 16) * 128:(tt_ - 15) * 128, :] if is_mem else x[tt_ * 128:(tt_ + 1) * 128, :]
                xtile, xb = xt.next()
                S.dma("sp", xtile[:], src, w=[xb])
                stb = Buf()
                c0 = tt_ * 4
                S.act(junk[:], xtile[:], AF.Square, r=[xb], w=[junkb, stb], accum_out=st[:, c0:c0 + 1])
                stt_[tt_] = (xtile, xb, stb, c0, is_mem)

            def stB(tt_):
                xtile, xb, stb, c0, is_mem = stt_[tt_]
                S.ts("dve", st[:, c0 + 1:c0 + 2], st[:, c0:c0 + 1], 1.0 / D, EPS, ALU.mult, ALU.add, r=[stb], w=[stb])
                S.act(st[:, c0 + 2:c0 + 3], st[:, c0 + 1:c0 + 2], AF.Sqrt, r=[stb], w=[stb])
                S.add("dve", lambda e, o=st[:, c0 + 3:c0 + 4], i=st[:, c0 + 2:c0 + 3]: e.reciprocal(out=o, in_=i), r=[stb], w=[stb])

            def stC(tt_):
                xtile, xb, stb, c0, is_mem = stt_[tt_]
                hbt, hbb = hb.next()
                S.stt(hbt[:], xtile[:], st[:, c0 + 3:c0 + 4], (gmrep if is_mem else grep)[:], ALU.mult, ALU.mult,
                      r=[xb, stb, gmb if is_mem else gb], w=[hbb])
                stt_[tt_] = (hbt, hbb, is_mem)

            def stD(tt_):
                hbt, hbb, is_mem = stt_[tt_]
                for g4 in range(4):
                    tpt, tpb = tp.next()
                    for k4 in range(4):
                        kc = g4 * 4 + k4
                        S.tr(tpt[:, k4 * 128:(k4 + 1) * 128], hbt[:, kc * 128:(kc + 1) * 128], ident[:], r=[hbb, identb], w=[tpb])
                    if is_mem:
                        dst = memT[:, g4 * 4:g4 * 4 + 4, (tt_ - 16) * 128:(tt_ - 15) * 128]
                        wbufs = []
                    else:
                        dst = hT[:, g4 * 4:g4 * 4 + 4, tt_ * 128:(tt_ + 1) * 128]
                        wbufs = [hTb[(tt_, g4)]]
                    S.copy("act" if cpc[0] % 2 == 0 else "dve", dst, tpt[:].rearrange("p (a b) -> p a b", a=4), r=[tpb], w=wbufs)
                    cpc[0] += 1

            blk = {}

            def gate_bank(gi, qb):
                if gi not in blk:
                    wt_, wb_ = ws1.get()
                    sgtile, sgb = sgt.next()
                    blk[gi] = (wt_, wb_, sgtile, sgb)
                wt_, wb_, sgtile, sgb = blk[gi]
                pt_, pb_ = pr.next()
                for kc in range(16):
                    S.mm(pt_[:], wt_[:, kc * 128:(kc + 1) * 128], hT[:, kc, qb * 512:(qb + 1) * 512], kc == 0, kc == 15,
                         r=[wb_] + [hTb[(4 * qb + i_, kc // 4)] for i_ in range(4)], w=[pb_])
                S.act(sgtile[:, qb * 512:(qb + 1) * 512], pt_[:], AF.Sigmoid, r=[pb_], w=[sgb])
                if qb == 3:
                    S.dma("sp", sg[gi % 16, :, :, gi // 16, :].rearrange("q p f -> p q f"), sgtile[:].rearrange("p (q f) -> p q f", q=4), r=[sgb])

            G = 6
            pending = [(gi, qb) for qb in range(4) for gi in range(G)]
            for step in range(18 + 3):
                if step < 18:
                    stA(step)
                if 0 <= step - 1 < 18:
                    stB(step - 1)
                if 0 <= step - 2 < 18:
                    stC(step - 2)
                if 0 <= step - 3 < 18:
                    stD(step - 3)
                for _ in range(3):
                    if pending and 4 * pending[0][1] + 3 <= step - 3:
                        gate_bank(*pending.pop(0))
            while pending:
                gate_bank(*pending.pop(0))
            ws1.depth = 6
            for gi in range(G, 48):
                if 8 <= gi < 24:
                    S.dma("pool", wo_scr[gi - 8], w_out[gi - 8])
                for qb in range(4):
                    gate_bank(gi, qb)
            S.emit()

        def perm_ap(t2d, qb, dl):
            if dl == 1:
                return t2d[:, qb * 512:(qb + 1) * 512]
            n_i = 512 // dl
            v = t2d.rearrange("p (r i) -> p r i", r=dl)[:, :, qb * n_i:(qb + 1) * n_i]
            return v.rearrange("p r i -> p i r")

        def nat_ap(t2d, dl):
            if dl == 1:
                return t2d
            return t2d.rearrange("p (i r) -> p i r", r=dl)

        with ExitStack() as pa_:
            cosT = sb(pa_, "cosT", [128, 2048], F32)
            sinX = sb(pa_, "sinX", [128, 2048], F32)
            bmask = sb(pa_, "bmask", [128, 1280], BF16)
            wr = Ring([sb(pa_, "wrA%d" % i, [128, 2048], BF16) for i in range(4)])
            ra = Ring([sb(pa_, "ra%d" % i, [128, 512], F32) for i in range(2)])
            rb = Ring([sb(pa_, "rb%d" % i, [128, 512], F32) for i in range(2)])
            pring = Ring([sb(pa_, "pt%d" % i, [128, 512], BF16) for i in range(4)])
            rl = Ring([sb(pa_, "rl%d" % i, [128, 512], F32) for i in range(2)])
            sgr = Ring([sb(pa_, "sgr%d" % i, [128, 512], F32) for i in range(2)])
            o32 = Ring([sb(pa_, "o32%d" % i, [128, 512], F32) for i in range(2)])
            ystage = Ring([sb(pa_, "ys%d" % i, [128, 2048], BF16) for i in range(2)])
            vTb = Buf()
            vTh = [None]
            constb = Buf()
            first_consts = [True]

            def load_consts():
                S.dma("sp", cosT[:], cos_d, w=[constb])
                S.dma("sp", sinX[:], sin_d, w=[constb])
                S.dma("pool", bmask[:], bmask_d, w=[constb])

            def rope_epi(dest, destb, dl):
                def epi(qb, pt_, pb_):
                    a, ab = ra.next()
                    b, bb = rb.next()
                    sl_ = slice(qb * 512, (qb + 1) * 512)
                    S.tt("dve", a[:], pt_[:], cosT[:, sl_], ALU.mult, r=[pb_, constb], w=[ab])
                    S.tt("dve", b[0:64, :], pt_[64:128, :], sinX[64:128, sl_], ALU.mult, r=[pb_, constb], w=[bb])
                    S.tt("dve", b[64:128, :], pt_[0:64, :], sinX[0:64, sl_], ALU.mult, r=[pb_, constb], w=[bb])
                    S.tt("pool", perm_ap(dest[:], qb, dl), nat_ap(a[:], dl), nat_ap(b[:], dl), ALU.add, r=[ab, bb], w=[destb])
                return epi

            def copy_epi(dest, destb, dl):
                def epi(qb, pt_, pb_):
                    S.copy("act", perm_ap(dest[:], qb, dl), nat_ap(pt_[:], dl), r=[pb_], w=[destb])
                return epi

            def silu_epi(dest, destb):
                def epi(qb, pt_, pb_):
                    t_, tb_ = sgr.next()
                    S.act(t_[:], pt_[:], AF.Exp, r=[pb_], w=[tb_], scale=-1.0)
                    S.act(t_[:], t_[:], AF.Ln, r=[tb_], w=[tb_], bias=1.0)
                    S.act(t_[:], t_[:], AF.Exp, r=[tb_], w=[tb_], scale=-1.0)
                    S.tt("dve", dest[:, qb * 512:(qb + 1) * 512], pt_[:], t_[:], ALU.mult, r=[pb_, tb_], w=[destb])
                return epi

            wsh = [None]
            srcs3 = []
            for j_ in range(4):
                for g_ in range(3):
                    srcs3 += [w_in[56 + g_ * 4 + j_], w_in[32 + g_ * 4 + j_], w_in[44 + g_ * 4 + j_]]
                srcs3.append(w_in[68 + j_])
            srcs4 = []
            for hm_ in range(4):
                for dc_ in range(2):
                    for kind_ in range(2):
                        srcs4.append(w_in[(72 if kind_ == 0 else 80) + hm_ * 2 + dc_])
                        if hm_ == 0:
                            b0_ = (dc_ * 2 + kind_) * 4
                            srcs4 += [w_kv[b_] for b_ in range(b0_, b0_ + 4)]
            ws3 = WStream(S, wr, srcs3, 3)
            ws4 = WStream(S, wr, srcs4, 3)

            def load_w(src):
                return wsh[0].get()

            def v_transposes(vtok, vtokb, vps, ntile=16):
                for g4 in range(ntile // 4):
                    vp, vpb = vps.next()
                    for k4 in range(4):
                        m = g4 * 4 + k4
                        S.tr(vp[:, k4 * 128:(k4 + 1) * 128], vTh[0][:, m * 128:(m + 1) * 128], ident[:], r=[vTb], w=[vpb])
                    S.copy("dve", vtok[:, g4 * 4:g4 * 4 + 4, :], vp[:].rearrange("p (a b) -> p a b", a=4), r=[vpb], w=[vtokb])

            def finish(Ob, Obb, Lb, Lbb, zT, zTb, qb, ys, ysb, zsl=None):
                r_, rlb = rl.next()
                o_, ob_ = o32.next()
                S.act(r_[:], Lb[:], AF.Ln, r=[Lbb], w=[rlb])
                S.act(r_[:], r_[:], AF.Exp, r=[rlb], w=[rlb], scale=-1.0)
                S.tt("dve", o_[:], Ob[:], r_[:], ALU.mult, r=[Obb, rlb], w=[ob_])
                zs = zT[:, qb * 512:(qb + 1) * 512] if zsl is None else zsl
                S.tt("pool", ys[:, qb * 512:(qb + 1) * 512], o_[:], zs, ALU.mult, r=[ob_, zTb], w=[ysb])

            with ExitStack() as pe_:
                sets = []
                for i in range(2):
                    sets.append(dict(
                        qT=sb(pe_, "qT%d" % i, [128, 2048], BF16), kT=sb(pe_, "kT%d" % i, [128, 2048], BF16),
                        vtok=sb(pe_, "vtok%d" % i, [128, 16, 128], BF16), zT=sb(pe_, "zT%d" % i, [128, 2048], F32),
                        nselT=sb(pe_, "nselT%d" % i, [128, 1024], BF16), kbar=sb(pe_, "kbar%d" % i, [128, 8], BF16),
                        qTb=Buf(), kTb=Buf(), vtokb=Buf(), zTb=Buf(), nselTb=Buf(), kbarb=Buf()))
                kb32 = sb(pe_, "kb32", [128, 8], F32)
                gm = sb(pe_, "gm", [128, 8, 8], F32)
                m8 = sb(pe_, "m8", [128, 8, 8], F32)
                ns1 = sb(pe_, "ns1", [128, 8, 8], F32)
                selbs = [Buf() for _ in range(8)]
                nsel = sb(pe_, "nsel", [128, 8, 128], F32)
                vT32 = sb(pe_, "vT32", [128, 2048], F32)
                ident32 = sb(pe_, "ident32", [128, 128], F32)
                vT32b = Buf()
                pr = Ring([ps(pe_, "prA%d" % i, [128, 512], F32) for i in range(2)])
                sring = Ring([ps(pe_, "sA%d" % i, [128, 512], F32) for i in range(3)])
                Obank = ps(pe_, "OA", [128, 512], F32)
                Lbank = ps(pe_, "LA", [128, 512], F32)
                gps = ps(pe_, "gpsA", [128, 64], F32)
                gpsb, nselb, Obb, Lbb, selb = [Buf() for _ in range(5)]
                cmask = sb(pe_, "cmask", [128, 128], BF16)
                negsel = sb(pe_, "negsel", [128, 1024], BF16)
                pastb = sb(pe_, "pastb", [128, 64], F32)
                pastm = sb(pe_, "pastm", [128, 64], F32)
                wsh[0] = WStream(S, wr, [w_in[b_] for h in range(8) for b_ in (16 + h, 24 + h, h, 8 + h)], 3)
                wsh[0].prime()
                load_consts()
                S.dma("pool", cmask[:], cmask_d[:, 0:128], w=[constb])
                S.dma("pool", negsel[:], negsel_d, w=[constb])
                S.dma("sp", pastb[:], pastb_d, w=[constb])
                S.dma("sp", pastm[:], pastm_d, w=[constb])
                S.dma("sp", ident32[:], ident_d, w=[constb])
                S.add("dve", lambda e: e.memset(nsel[:], 0.0), w=[nselb])
                scale = 128.0 ** -0.5

                def proj_gen(h, st):
                    for which in ("v", "z", "q", "k"):
                        wt_, wb_ = wsh[0].get()
                        if which == "q":
                            epi = rope_epi(st["qT"], st["qTb"], 1)
                        elif which == "k":
                            epi = rope_epi(st["kT"], st["kTb"], 1)
                        elif which == "v":
                            epi = copy_epi(vT32, vT32b, 1)
                        else:
                            epi = silu_epi(st["zT"], st["zTb"])
                        for qb in range(4):
                            pt_, pb_ = pr.next()
                            for kc in range(16):
                                S.mm(pt_[:], wt_[:, kc * 128:(kc + 1) * 128], hT[:, kc, qb * 512:(qb + 1) * 512], kc == 0, kc == 15,
                                     r=[wb_], w=[pb_])
                            epi(qb, pt_, pb_)
                            yield
                        if which == "z":
                            for g4 in range(4):
                                vp, vpb = pr.next()
                                for k4 in range(4):
                                    m = g4 * 4 + k4
                                    S.tr(vp[:, k4 * 128:(k4 + 1) * 128], vT32[:, m * 128:(m + 1) * 128], ident32[:], r=[vT32b, constb], w=[vpb])
                                S.copy("dve", st["vtok"][:, g4 * 4:g4 * 4 + 4, :], vp[:].rearrange("p (a b) -> p a b", a=4), r=[vpb], w=[st["vtokb"]])
                            yield
                def sel_gen(h, st):
                    kT, qT, kbar = st["kT"], st["qT"], st["kbar"]
                    S.add("dve", lambda e: e.tensor_reduce(out=kb32[:], in_=kT[:].rearrange("p (j t) -> p j t", j=8), axis=AX.X, op=ALU.add),
                          r=[st["kTb"]], w=[st["kbarb"]])
                    S.ts("dve", kbar[:], kb32[:], 1.0 / 256.0, None, ALU.mult, None, r=[st["kbarb"]], w=[st["kbarb"]])
                    yield
                    yield
                    for t8 in range(8):
                        S.mm(gps[:, t8 * 8:(t8 + 1) * 8], qT[:, (8 + t8) * 128:(9 + t8) * 128], kbar[:], True, True,
                             r=[st["qTb"], st["kbarb"]], w=[gpsb])
                    yield
                    for tp_ in range(0, 8, 4):
                        grp = list(range(tp_, tp_ + 4))
                        for t8 in grp:
                            jb = (8 + t8) // 2
                            S.tt("dve", gm[:, t8, :], gps[:, t8 * 8:(t8 + 1) * 8], pastb[:, jb * 8:(jb + 1) * 8], ALU.add,
                                 r=[gpsb, constb], w=[selbs[t8]])
                        for t8 in grp:
                            S.add("dve", lambda e, t8=t8: e.max(out=m8[:, t8, :], in_=gm[:, t8, :]), r=[selbs[t8]], w=[selbs[t8]])
                        for t8 in grp:
                            S.ts("dve", ns1[:, t8, :], gm[:, t8, :], m8[:, t8, 2:3], None, ALU.is_lt, None, r=[selbs[t8]], w=[selbs[t8]])
                        for t8 in grp:
                            jb = (8 + t8) // 2
                            S.tt("dve", nsel[:, t8, 0:8], ns1[:, t8, :], pastm[:, jb * 8:(jb + 1) * 8], ALU.mult,
                                 r=[selbs[t8], constb], w=[nselb])
                        yield
                    for g2_ in range(2):
                        vp, vpb = pr.next()
                        for k4 in range(4):
                            t8 = g2_ * 4 + k4
                            S.tr(vp[:, k4 * 128:(k4 + 1) * 128], nsel[:, t8, :], ident32[:], r=[nselb, constb], w=[vpb])
                        S.copy("act", st["nselT"][:, g2_ * 512:(g2_ + 1) * 512], vp[:], r=[vpb], w=[st["nselTb"]])
                        yield

                def attn_gen(h, st):
                    qT, kT, vtok, nselT = st["qT"], st["kT"], st["vtok"], st["nselT"]
                    ys, ysb = ystage.next()
                    tiles = [(qb, kt) for qb in range(4) for kt in range(4 * qb + 4)]

                    def s_tile(qb, kt):
                        sp_, sb_ = sring.next()
                        m_sel = qb >= 2 and kt <= 4 * qb + 1
                        m_c = kt >= 4 * qb
                        c0 = (kt - 4 * qb) * 128 if m_c else 0
                        S.mm(sp_[:, c0:512], kT[:, kt * 128:(kt + 1) * 128], qT[:, qb * 512 + c0:(qb + 1) * 512], True, not (m_sel or m_c),
                             r=[st["kTb"], st["qTb"]], w=[sb_], skip=True)
                        if m_sel:
                            j = kt // 2
                            S.mm(sp_[:, c0:512], negsel[:, j * 128:(j + 1) * 128], nselT[:, (qb - 2) * 512 + c0:(qb - 1) * 512], False, not m_c,
                                 r=[st["nselTb"], constb], w=[sb_], skip=True)
                        if m_c:
                            S.mm(sp_[:, c0:c0 + 128], ident[:], cmask[:, 0:128], False, True, r=[constb], w=[sb_], skip=True)
                        return sp_, sb_, c0
                    sts = {}
                    pts = {}

                    def do_exp(i):
                        sp_, sb_, c0 = sts.pop(i)
                        p_, pb_ = pring.next()
                        S.act(p_[:, c0:512], sp_[:, c0:512], AF.Exp, r=[sb_], w=[pb_], scale=scale)
                        pts[i] = (p_, pb_, c0)
                    n_t = len(tiles)
                    sts[0] = s_tile(*tiles[0])
                    sts[1] = s_tile(*tiles[1])
                    do_exp(0)
                    do_exp(1)
                    for i, (qb, kt) in enumerate(tiles):
                        if i + 2 < n_t:
                            sts[i + 2] = s_tile(*tiles[i + 2])
                            do_exp(i + 2)
                        nk = 4 * qb + 4
                        p_, pb_, c0 = pts.pop(i)
                        S.mm(Obank[:, c0:512], vtok[:, kt, :], p_[:, c0:512], kt == 0, kt == nk - 1, r=[pb_, st["vtokb"]], w=[Obb], skip=True)
                        S.mm(Lbank[:, c0:512], ones[:], p_[:, c0:512], kt == 0, kt == nk - 1, r=[pb_], w=[Lbb], skip=True)
                        yield "tile"
                        if kt == nk - 1:
                            finish(Obank, Obb, Lbank, Lbb, st["zT"], st["zTb"], qb, ys, ysb)
                            yield "fin"
                    S.dma("sp", yT[h], ys[:], r=[ysb])

                for _ in proj_gen(0, sets[0]):
                    pass
                for h in range(8):
                    if h == 7:
                        ws3.prime()
                    sg_ = sel_gen(h, sets[h % 2])
                    pg = proj_gen(h + 1, sets[(h + 1) % 2]) if h + 1 < 8 else None
                    k_ = 0
                    for tag in attn_gen(h, sets[h % 2]):
                        k_ += 1
                        if sg_ is not None:
                            if next(sg_, "done") == "done":
                                sg_ = None
                            if tag != "fin":
                                continue
                        if pg is not None and (tag == "fin" or k_ % 2 == 0):
                            if next(pg, "done") == "done":
                                pg = None
                    if pg is not None:
                        for _ in pg:
                            pass
                ws3.prime()
                S.emit()

            with ExitStack() as pe_:
                DL = (1, 4, 16)
                qTg = [sb(pe_, "qTg%d" % g, [128, 2048], BF16) for g in range(3)]
                kTg = [sb(pe_, "kTg%d" % g, [128, 2048], BF16) for g in range(3)]
                vtg = [sb(pe_, "vtg%d" % g, [128, 16, 128], BF16) for g in range(3)]
                zT = sb(pe_, "zTB", [128, 2048], F32)
                PT2 = sb(pe_, "PT2", [128, 2048], BF16)
                vT = sb(pe_, "vTB", [128, 2048], BF16)
                vTh[0] = vT
                pr = Ring([ps(pe_, "prB%d" % i, [128, 512], F32) for i in range(2)])
                sring = Ring([ps(pe_, "sB%d" % i, [128, 512], F32) for i in range(3)])
                Obank = ps(pe_, "OB", [128, 512], F32)
                Lbank = ps(pe_, "LB", [128, 512], F32)
                vps = Ring([ps(pe_, "vpsB", [128, 512], BF16)])
                qgb = [Buf() for _ in range(3)]
                kgb = [Buf() for _ in range(3)]
                vgb = [Buf() for _ in range(3)]
                zTb, PT2b, Obb, Lbb = Buf(), Buf(), Buf(), Buf()
                scale = 128.0 ** -0.5
                band = bmask[:, 0:256]
                band4c = bmask[:, 256:768]
                band4p = bmask[:, 768:1280]
                wsh[0] = ws3
                g0 = [dict(q=qTg[0], k=kTg[0], v=vtg[0], qb=qgb[0], kb=kgb[0], vb=vgb[0]),
                      dict(q=sb(pe_, "qTg0b", [128, 2048], BF16), k=sb(pe_, "kTg0b", [128, 2048], BF16),
                           v=sb(pe_, "vtg0b", [128, 16, 128], BF16), qb=Buf(), kb=Buf(), vb=Buf())]

                def bank_gen(wt_, wb_, epi, hook=None):
                    for qb in range(4):
                        pt_, pb_ = pr.next()
                        for kc in range(16):
                            S.mm(pt_[:], wt_[:, kc * 128:(kc + 1) * 128], hT[:, kc, qb * 512:(qb + 1) * 512], kc == 0, kc == 15,
                                 r=[wb_], w=[pb_])
                        epi(qb, pt_, pb_)
                        if qb == 0 and hook is not None:
                            hook()
                        yield

                def projB_gen(j, part):
                    for g in ((0,) if part == "g0" else (1, 2)):
                        if g == 0:
                            st0 = g0[j % 2]
                            qd, kd, vd, qdb, kdb, vdb = st0["q"], st0["k"], st0["v"], st0["qb"], st0["kb"], st0["vb"]
                        else:
                            qd, kd, vd, qdb, kdb, vdb = qTg[g], kTg[g], vtg[g], qgb[g], kgb[g], vgb[g]
                        wt_, wb_ = load_w(None)
                        yield from bank_gen(wt_, wb_, copy_epi(vT, vTb, DL[g]))
                        wt_, wb_ = load_w(None)
                        yield from bank_gen(wt_, wb_, rope_epi(qd, qdb, DL[g]), hook=lambda vd=vd, vdb=vdb: v_transposes(vd, vdb, vps))
                        wt_, wb_ = load_w(None)
                        yield from bank_gen(wt_, wb_, rope_epi(kd, kdb, DL[g]))
                    if part == "rest":
                        wt_, wb_ = load_w(None)
                        yield from bank_gen(wt_, wb_, silu_epi(zT, zTb))
                        for rb4 in range(4):
                            sp_, sb_ = sring.next()
                            for rr in range(4):
                                r = rb4 * 4 + rr
                                S.mm(sp_[:, rr * 128:(rr + 1) * 128], kTg[2][:, r * 128:(r + 1) * 128], qTg[2][:, r * 128:(r + 1) * 128],
                                     rr == 0, False, r=[kgb[2], qgb[2]], w=[sb_], skip=True)
                            S.mm(sp_[:], ident[:], band4c, False, True, r=[constb], w=[sb_], skip=True)
                            S.act(PT2[:, rb4 * 512:(rb4 + 1) * 512], sp_[:], AF.Exp, r=[sb_], w=[PT2b], scale=scale)
                        yield

                def attnB_gen(j):
                    st0 = g0[j % 2]
                    q0, k0, v0, q0b, k0b, v0b = st0["q"], st0["k"], st0["v"], st0["qb"], st0["kb"], st0["vb"]
                    ys, ysb = ystage.next()
                    for qb in range(4):
                        first = [True]

                        def pv(ocols_fn, vt, vtb, pap, rbufs):
                            st_ = first[0]
                            first[0] = False
                            S.mm(ocols_fn(Obank), vt, pap, st_, False, r=rbufs + [vtb], w=[Obb], skip=True)
                            S.mm(ocols_fn(Lbank), ones[:], pap, st_, False, r=rbufs, w=[Lbb], skip=True)
                        tiles = []
                        for kt in range(max(0, 4 * qb - 1), 4 * qb + 4):
                            qlo = max(kt, 4 * qb)
                            qhi = min(kt + 1, 4 * qb + 3)
                            ncol = (qhi - qlo + 1) * 128
                            ms = 0 if qlo == kt else 128
                            oc0 = (qlo - 4 * qb) * 128

                            def fs(sp_, sb_, kt=kt, qlo=qlo, qhi=qhi, ncol=ncol, ms=ms):
                                S.mm(sp_[:, 0:ncol], k0[:, kt * 128:(kt + 1) * 128], q0[:, qlo * 128:(qhi + 1) * 128], True, False,
                                     r=[k0b, q0b], w=[sb_])
                                S.mm(sp_[:, 0:ncol], ident[:], band[:, ms:ms + ncol], False, True, r=[constb], w=[sb_])

                            def fp(p_, pb_, kt=kt, ncol=ncol, oc0=oc0):
                                pv(lambda bk: bk[:, oc0:oc0 + ncol], v0[:, kt, :], v0b, p_[:, 0:ncol], [pb_])
                            tiles.append((fs, ncol, fp))
                        for kt in (qb - 1, qb):
                            if kt < 0:
                                continue

                            def fs(sp_, sb_, kt=kt, qb=qb):
                                for r in range(4):
                                    S.mm(sp_[:, r * 128:(r + 1) * 128], kTg[1][:, r * 512 + kt * 128:r * 512 + (kt + 1) * 128],
                                         qTg[1][:, r * 512 + qb * 128:r * 512 + (qb + 1) * 128], r == 0, False,
                                         r=[kgb[1], qgb[1]], w=[sb_], skip=True)
                                S.mm(sp_[:], ident[:], band4c if kt == qb else band4p, False, True, r=[constb], w=[sb_], skip=True)

                            def fp(p_, pb_, kt=kt):
                                for r in range(4):
                                    pv(lambda bk, r=r: bk[:].rearrange("p (i r) -> p i r", r=4)[:, :, r], vtg[1][:, r * 4 + kt, :], vgb[1],
                                       p_[:, r * 128:(r + 1) * 128], [pb_])
                            tiles.append((fs, 512, fp))
                        banks = {}

                        def issue(i):
                            sp_, sb_ = sring.next()
                            tiles[i][0](sp_, sb_)
                            banks[i] = (sp_, sb_)
                        pts = {}

                        def do_exp(i):
                            sp_, sb_ = banks.pop(i)
                            ncol = tiles[i][1]
                            p_, pb_ = pring.next()
                            S.act(p_[:, 0:ncol], sp_[:, 0:ncol], AF.Exp, r=[sb_], w=[pb_], scale=scale)
                            pts[i] = (p_, pb_)
                        issue(0)
                        do_exp(0)
                        if len(tiles) > 1:
                            issue(1)
                            do_exp(1)
                        for i in range(len(tiles)):
                            if i + 2 < len(tiles):
                                issue(i + 2)
                                do_exp(i + 2)
                            p_, pb_ = pts.pop(i)
                            tiles[i][2](p_, pb_)
                            yield "tile"
                        for r in range(16):
                            pv(lambda bk, r=r: bk[:].rearrange("p (i r) -> p i r", r=16)[:, :, r], vtg[2][:, r, :], vgb[2],
                               PT2[:, r * 128 + 32 * qb:r * 128 + 32 * qb + 32], [PT2b])
                        finish(Obank, Obb, Lbank, Lbb, zT, zTb, qb, ys, ysb)
                        yield "fin"
                    S.dma("sp", yT[8 + j], ys[:], r=[ysb])

                for _ in projB_gen(0, "g0"):
                    pass
                for _ in projB_gen(0, "rest"):
                    pass
                for j in range(4):
                    if j == 3:
                        ws4.prime()
                    pg = projB_gen(j + 1, "g0") if j + 1 < 4 else None
                    k_ = 0
                    for tag in attnB_gen(j):
                        k_ += 1
                        if pg is not None and (tag == "fin" or k_ % 2 == 0):
                            if next(pg, "done") == "done":
                                pg = None
                    if pg is not None:
                        for _ in pg:
                            pass
                    if j + 1 < 4:
                        for _ in projB_gen(j + 1, "rest"):
                            pass
                ws4.prime()
                S.emit()

            with ExitStack() as pe_:
                mkT = sb(pe_, "mkT", [128, 8, 256], BF16)
                vT = sb(pe_, "vTM", [128, 2048], BF16)
                vTh[0] = vT
                mvtok = sb(pe_, "mvtok", [128, 2, 1024], BF16)
                qmT = [sb(pe_, "qmT%d" % i, [128, 2048], BF16) for i in range(2)]
                zmT = [sb(pe_, "zmT%d" % i, [128, 2048], F32) for i in range(2)]
                pr = Ring([ps(pe_, "prM%d" % i, [128, 512], F32) for i in range(2)])
                sring = Ring([ps(pe_, "sM%d" % i, [128, 512], F32) for i in range(2)])
                Ob2 = [ps(pe_, "OM%d" % i, [128, 512], F32) for i in range(2)]
                Lbank = ps(pe_, "LM", [128, 512], F32)
                vps = Ring([ps(pe_, "vpsM", [128, 512], BF16)])
                mkb, mvb, Lbb = Buf(), Buf(), Buf()
                Obb2 = [Buf(), Buf()]
                qmb = [Buf(), Buf()]
                zmb = [Buf(), Buf()]
                m_rhs = lambda kc, qb: memT[:, kc, :]
                wsh[0] = ws4
                def kv_job(kb):
                    wt_, wb_ = load_w(None)
                    if kb < 8:
                        def epi(qb, pt_, pb_):
                            S.copy("act", mkT[:, kb, :], pt_[:, 0:256], r=[pb_], w=[mkb])
                        proj_block(wt_, wb_, m_rhs, 1, 256, pr, epi)
                    else:
                        blk = kb - 8

                        def epi(qb, pt_, pb_):
                            S.copy("act", vT[:, 0:256], pt_[:, 0:256], r=[pb_], w=[vTb])
                        proj_block(wt_, wb_, m_rhs, 1, 256, pr, epi)
                        vp, vpb = vps.next()
                        for mt in range(2):
                            S.tr(vp[:, mt * 128:(mt + 1) * 128], vT[:, mt * 128:(mt + 1) * 128], ident[:], r=[vTb], w=[vpb])
                        S.copy("dve", mvtok[:, :, blk * 128:(blk + 1) * 128], vp[:, 0:256].rearrange("p (a b) -> p a b", a=2), r=[vpb], w=[mvb])
                scale = 256.0 ** -0.5
                msets = []
                for i in range(2):
                    msets.append(dict(qm=qmT if i == 0 else [sb(pe_, "qmTb%d" % k, [128, 2048], BF16) for k in range(2)],
                                      zm=zmT if i == 0 else [sb(pe_, "zmTb%d" % k, [128, 2048], F32) for k in range(2)],
                                      qmb=[Buf(), Buf()], zmb=[Buf(), Buf()]))

                def projM_gen(hm, st):
                    for dc in range(2):
                        for kind in ("q", "z"):
                            wt_, wb_ = wsh[0].get()
                            epi = copy_epi(st["qm"][dc], st["qmb"][dc], 1) if kind == "q" else silu_epi(st["zm"][dc], st["zmb"][dc])
                            for qb in range(4):
                                pt_, pb_ = pr.next()
                                for kc in range(16):
                                    S.mm(pt_[:], wt_[:, kc * 128:(kc + 1) * 128], hT[:, kc, qb * 512:(qb + 1) * 512], kc == 0, kc == 15,
                                         r=[wb_], w=[pb_])
                                epi(qb, pt_, pb_)
                                yield

                def attnM_gen(hm, st):
                    ysl = [ystage.next(), ystage.next()]
                    for qb in range(4):
                        for mt in range(2):
                            sp_, sb_ = sring.next()
                            for dc in range(2):
                                S.mm(sp_[:], mkT[:, hm * 2 + dc, mt * 128:(mt + 1) * 128], st["qm"][dc][:, qb * 512:(qb + 1) * 512],
                                     dc == 0, dc == 1, r=[mkb, st["qmb"][dc]], w=[sb_])
                            p_, pb_ = pring.next()
                            S.act(p_[:], sp_[:], AF.Exp, r=[sb_], w=[pb_], scale=scale)
                            yield
                            for dvc in range(2):
                                S.mm(Ob2[dvc][:], mvtok[:, mt, hm * 256 + dvc * 128:hm * 256 + (dvc + 1) * 128], p_[:], mt == 0, mt == 1,
                                     r=[pb_, mvb], w=[Obb2[dvc]])
                            S.mm(Lbank[:], ones[:], p_[:], mt == 0, mt == 1, r=[pb_], w=[Lbb])
                        r_, rlb = rl.next()
                        S.act(r_[:], Lbank[:], AF.Ln, r=[Lbb], w=[rlb])
                        S.act(r_[:], r_[:], AF.Exp, r=[rlb], w=[rlb], scale=-1.0)
                        for dvc in range(2):
                            o_, ob_ = o32.next()
                            S.tt("dve", o_[:], Ob2[dvc][:], r_[:], ALU.mult, r=[Obb2[dvc], rlb], w=[ob_])
                            S.tt("pool", ysl[dvc][0][:, qb * 512:(qb + 1) * 512], o_[:], st["zm"][dvc][:, qb * 512:(qb + 1) * 512], ALU.mult,
                                 r=[ob_, st["zmb"][dvc]], w=[ysl[dvc][1]])
                        yield
                    for dvc in range(2):
                        S.dma("sp", yT[12 + hm * 2 + dvc], ysl[dvc][0][:], r=[ysl[dvc][1]])

                nb_ = 0
                for _ in projM_gen(0, msets[0]):
                    nb_ += 1
                    if nb_ % 4 == 0:
                        for kb in range(nb_ - 4, nb_):
                            kv_job(kb)
                for hm in range(4):
                    pg = projM_gen(hm + 1, msets[(hm + 1) % 2]) if hm + 1 < 4 else None
                    for _ in attnM_gen(hm, msets[hm % 2]):
                        if pg is not None:
                            if next(pg, "done") == "done":
                                pg = None
                    if pg is not None:
                        for _ in pg:
                            pass
                S.emit()

        es_h.close()

        with ExitStack() as pm_:
            mergedT = sb(pm_, "mergedT", [128, 16, 2048], BF16)
            with ExitStack() as pe_:
                ysb_ = sb(pe_, "yT_sb", [128, 20, 2048], BF16)
                wab = Ring([sb(pe_, "wab%d" % i, [128, 1536], BF16) for i in range(2)])
                wm = Ring([sb(pe_, "wm%d" % i, [128, 1024], BF16) for i in range(2)])
                gr = Ring([sb(pe_, "gr%d" % i, [128, 3, 512], BF16) for i in range(4)])
                t1r = Ring([sb(pe_, "t1_%d" % i, [128, 512], F32) for i in range(2)])
                t2r = Ring([sb(pe_, "t2_%d" % i, [128, 512], F32) for i in range(2)])
                t3r = Ring([sb(pe_, "t3_%d" % i, [128, 512], F32) for i in range(2)])
                s12 = Ring([sb(pe_, "s12_%d" % i, [128, 512], F32) for i in range(2)])
                PAr = Ring([ps(pe_, "PA%d" % i, [128, 512], F32) for i in range(2)])
                PBr = Ring([ps(pe_, "PB%d" % i, [128, 512], F32) for i in range(2)])
                PMr = Ring([ps(pe_, "PM%d" % i, [128, 512], F32) for i in range(2)])
                yb_ = [Buf() for _ in range(20)]
                for i in range(20):
                    S.dma("sp", ysb_[:, i, :], yT[i], w=[yb_[i]])
                def issue_w(c):
                    wa_, wab_b = wab.next()
                    wm_, wm_b = wm.next()
                    S.dma("pool", wa_[:], w_pab[c], w=[wab_b])
                    S.dma("pool", wm_[:], w_pm[c], w=[wm_b])
                    return wa_, wab_b, wm_, wm_b
                pend = [issue_w(0)]
                for c in range(16):
                    if c + 1 < 16:
                        pend.append(issue_w(c + 1))
                    wa_, wab_b, wm_, wm_b = pend[c]
                    for qb in range(4):
                        g_, g_b = gr.next()
                        S.dma("sp", g_[:], sg[c, qb], w=[g_b])
                        qs = slice(qb * 512, (qb + 1) * 512)
                        pa, pab = PAr.next()
                        for ec in range(8):
                            S.mm(pa[:], wa_[:, ec * 128:(ec + 1) * 128], ysb_[:, ec, qs], ec == 0, ec == 7, r=[wab_b, yb_[ec]], w=[pab])
                        pb, pbb = PBr.next()
                        for ec in range(4):
                            S.mm(pb[:], wa_[:, (8 + ec) * 128:(9 + ec) * 128], ysb_[:, 8 + ec, qs], ec == 0, ec == 3, r=[wab_b, yb_[8 + ec]], w=[pbb])
                        pm, pmb = PMr.next()
                        for ec in range(8):
                            S.mm(pm[:], wm_[:, ec * 128:(ec + 1) * 128], ysb_[:, 12 + ec, qs], ec == 0, ec == 7, r=[wm_b, yb_[12 + ec]], w=[pmb])
                        t1, t1b = t1r.next()
                        t2, t2b = t2r.next()
                        t3, t3b = t3r.next()
                        s_, s_b = s12.next()
                        S.tt("dve", t1[:], pa[:], g_[:, 0, :], ALU.mult, r=[pab, g_b], w=[t1b])
                        S.tt("dve", t2[:], pb[:], g_[:, 1, :], ALU.mult, r=[pbb, g_b], w=[t2b])
                        S.tt("dve", t3[:], pm[:], g_[:, 2, :], ALU.mult, r=[pmb, g_b], w=[t3b])
                        S.tt("pool", s_[:], t1[:], t2[:], ALU.add, r=[t1b, t2b], w=[s_b])
                        S.tt("pool", mergedT[:, c, qs], s_[:], t3[:], ALU.add, r=[s_b, t3b])
                S.emit()

            with ExitStack() as pe_:
                wo = sb(pe_, "wo", [128, 16, 2048], BF16)
                gf = sb(pe_, "gf", [128, 2048], F32)
                xt = Ring([sb(pe_, "xf%d" % i, [128, 2048], F32) for i in range(2)])
                rt = Ring([sb(pe_, "rf%d" % i, [128, 2048], F32) for i in range(2)])
                ot = Ring([sb(pe_, "of%d" % i, [128, 2048], F32) for i in range(2)])
                junk = sb(pe_, "junkf", [128, 2048], BF16)
                st = sb(pe_, "stf", [128, 128], F32)
                pr = Ring([ps(pe_, "pf%d" % i, [128, 512], F32) for i in range(8)])
                wob = [Buf() for _ in range(16)]
                gfb, junkb = Buf(), Buf()
                obcs = [[Buf() for _ in range(4)] for _ in range(2)]
                for dc in range(16):
                    S.dma("sp", wo[:, dc, :], wo_scr[dc], w=[wob[dc]])
                S.dma("sp", gf[:], g_fin, w=[gfb])
                for tt_ in range(16):
                    xtile, xb = xt.next()
                    S.dma("sp", xtile[:], x[tt_ * 128:(tt_ + 1) * 128, :], w=[xb])
                    r_, rb_ = rt.next()
                    stb = Buf()
                    c0 = tt_ * 8
                    rbs = [Buf() for _ in range(4)]
                    for cb in range(4):
                        p_, pb_ = pr.next()
                        cs = slice(cb * 512, (cb + 1) * 512)
                        for dc in range(16):
                            S.mm(p_[:], mergedT[:, dc, tt_ * 128:(tt_ + 1) * 128], wo[:, dc, cs], dc == 0, dc == 15,
                                 r=[wob[dc]], w=[pb_])
                        S.tt("dve", r_[:, cs], p_[:], xtile[:, cs], ALU.add, r=[pb_, xb], w=[rb_, rbs[cb]])
                        S.act(junk[:, cs], r_[:, cs], AF.Square, r=[rbs[cb]], w=[junkb, stb], accum_out=st[:, c0 + cb:c0 + cb + 1])
                    S.add("dve", lambda e, o=st[:, c0 + 4:c0 + 5], i=st[:, c0:c0 + 4]: e.tensor_reduce(out=o, in_=i, axis=AX.X, op=ALU.add),
                          r=[stb], w=[stb])
                    S.ts("dve", st[:, c0 + 5:c0 + 6], st[:, c0 + 4:c0 + 5], 1.0 / D, EPS, ALU.mult, ALU.add, r=[stb], w=[stb])
                    S.act(st[:, c0 + 6:c0 + 7], st[:, c0 + 5:c0 + 6], AF.Sqrt, r=[stb], w=[stb])
                    S.add("dve", lambda e, o=st[:, c0 + 7:c0 + 8], i=st[:, c0 + 6:c0 + 7]: e.reciprocal(out=o, in_=i), r=[stb], w=[stb])
                    o_, ob_ = ot.next()
                    for cb in range(4):
                        cs = slice(cb * 512, (cb + 1) * 512)
                        obc = obcs[tt_ % 2][cb]
                        S.stt(o_[:, cs], r_[:, cs], st[:, c0 + 7:c0 + 8], gf[:, cs], ALU.mult, ALU.mult, r=[rb_, stb, gfb], w=[obc])
                        S.dma("sp", out[tt_ * 128:(tt_ + 1) * 128, cs], o_[:, cs], r=[obc])
                S.emit()
    return nc


def _host_consts():
    half = 64
    inv = 10000.0 ** (-np.arange(half, dtype=np.float64) / float(half))
    ang = np.arange(2048, dtype=np.float64)[:, None] * inv[None, :]
    cos = np.cos(ang).astype(np.float32).T
    sin = np.sin(ang).astype(np.float32).T
    cosT = np.concatenate([cos, cos], axis=0)
    sinX = np.concatenate([sin, -sin], axis=0)
    kp = np.arange(128)[:, None]
    cm = []
    for o in range(4):
        qf = np.arange(512)[None, :]
        cm.append(np.where(o * 128 + kp <= qf, 0.0, NEG))
    cmask = np.concatenate(cm, axis=1).astype(np.float32)
    qf = np.arange(256)[None, :]
    dlt = qf - kp
    band = np.where((dlt >= 0) & (dlt <= 128), 0.0, NEG).astype(np.float32)
    bmask = np.concatenate([band, np.tile(band[:, 0:128], (1, 4)), np.tile(band[:, 128:256], (1, 4))], axis=1).astype(np.float32)
    ident = np.eye(128, dtype=np.float32)
    negsel = np.zeros((128, 8, 128), np.float32)
    for j in range(8):
        negsel[j, j, :] = NEG
    negsel = negsel.reshape(128, 1024)
    pastb = np.zeros((128, 8, 8), np.float32)
    pastm = np.zeros((128, 8, 8), np.float32)
    for jb in range(8):
        for j in range(8):
            pastb[:, jb, j] = 0.0 if j < jb else -1e30
            pastm[:, jb, j] = 1.0 if j < jb else 0.0
    return dict(cosT=np.ascontiguousarray(cosT), sinX=np.ascontiguousarray(sinX), cmask=cmask, bmask=bmask, ident=ident,
                negsel=negsel, pastb=pastb.reshape(128, 64), pastm=pastm.reshape(128, 64))


def _blk(w, nb):
    return np.ascontiguousarray(w.reshape(16, 128, nb, 128).transpose(2, 1, 0, 3).reshape(nb, 128, 2048))


def _host_layout(inputs):
    f = lambda a: np.ascontiguousarray(np.asarray(a, dtype=np.float32))
    w_in = _blk(f(inputs["w_in"])[0], 136)
    w_kv = _blk(f(inputs["w_mem_kv"])[0], 16)
    wa = f(inputs["w_proj_a"])[0].reshape(8, 128, 16, 128).transpose(2, 1, 0, 3)
    wb = f(inputs["w_proj_b"])[0].reshape(4, 128, 16, 128).transpose(2, 1, 0, 3)
    wm = f(inputs["w_proj_m"])[0].reshape(8, 128, 16, 128).transpose(2, 1, 0, 3)
    w_pab = np.ascontiguousarray(np.concatenate([wa, wb], axis=2).reshape(16, 128, 1536))
    w_pm = np.ascontiguousarray(wm.reshape(16, 128, 1024))
    w_out = np.ascontiguousarray(f(inputs["w_out"])[0].reshape(16, 128, 2048))
    rep = lambda g: np.ascontiguousarray(np.broadcast_to(f(g).reshape(1, 2048), (128, 2048)))
    shared = dict(w_in=w_in, w_kv=w_kv, w_pab=w_pab, w_pm=w_pm, w_out=w_out,
                  g_in=rep(inputs["norm_in_g"]), g_mem=rep(inputs["norm_mem_g"]), g_fin=rep(inputs["norm_final_g"]))
    shared.update(_host_consts())
    return shared


_NC_CACHE = {}


def kernel(x, mem, norm_in_g, norm_mem_g, w_in, w_mem_kv, w_proj_a, w_proj_b, w_proj_m, w_out, norm_final_g):
    inputs = dict(x=x, mem=mem, norm_in_g=norm_in_g, norm_mem_g=norm_mem_g, w_in=w_in, w_mem_kv=w_mem_kv,
                  w_proj_a=w_proj_a, w_proj_b=w_proj_b, w_proj_m=w_proj_m, w_out=w_out, norm_final_g=norm_final_g)
    shared = _host_layout(inputs)
    xs = np.asarray(x, dtype=np.float32)
    ms = np.asarray(mem, dtype=np.float32)
    n = 8
    nc = build_nc()
    in_maps = []
    for b in range(n):
        m = dict(shared)
        m["x"] = np.ascontiguousarray(xs[b])
        m["mem"] = np.ascontiguousarray(ms[b])
        in_maps.append(m)
    res = run_bass_kernel_spmd(nc, in_maps, core_ids=list(range(n)))
    return np.stack([np.asarray(r["out"], dtype=np.float32) for r in res.results], axis=0)
```

```python
import numpy as np
from contextlib import ExitStack
import concourse.bass as bass
import concourse.mybir as mybir
from concourse.bass_utils import run_bass_kernel_spmd

F32 = mybir.dt.float32
BF16 = mybir.dt.bfloat16
AF = mybir.ActivationFunctionType
ALU = mybir.AluOpType
AX = mybir.AxisListType

ENG = ("pe", "act", "dve", "pool", "sp")
BLK = {"pe": "tensor", "act": "scalar", "dve": "vector", "pool": "gpsimd", "sp": "sync"}
DMAQ = ("sp", "pool")
NR = 12
SAME_ENGINE_SYNC = True
NEG = -30000.0
D = 2048
S_LEN = 2048
EPS = 1e-6


class Op:
    __slots__ = ("eng", "fn", "deps", "sig", "ev", "dma", "ph")


class Buf:
    __slots__ = ("lw", "rd")

    def __init__(self):
        self.lw = None
        self.rd = []


class Sched:
    def __init__(self, nc, es):
        self.nc = nc
        self.esem = {e: es.enter_context(nc.semaphore("s_" + e)) for e in ENG}
        self.ecnt = {e: 0 for e in ENG}
        self.dq = {q: [es.enter_context(nc.semaphore("d_%s%d" % (q, i))) for i in range(NR)] for q in DMAQ}
        self.dn = {q: 0 for q in DMAQ}
        self.dlast = {q: [None] * NR for q in DMAQ}
        self.phase_sem = es.enter_context(nc.semaphore("phase"))
        self.phase_n = 0
        self.waited = {e: {} for e in ENG}
        self.ops = {e: [] for e in ENG}

    def add(self, eng, fn, r=(), w=(), dma=False):
        op = Op()
        op.eng = eng
        op.fn = fn
        op.dma = dma
        op.sig = False
        op.ev = None
        op.ph = self.phase_n
        deps = []
        for b in r:
            if b.lw is not None:
                deps.append(b.lw)
        for b in w:
            if b.lw is not None:
                deps.append(b.lw)
            deps.extend(b.rd)
        if dma:
            n = self.dn[eng]
            slot = n % NR
            prev = self.dlast[eng][slot]
            if prev is not None:
                deps.append(prev)
            self.dlast[eng][slot] = op
            self.dn[eng] = n + 1
            op.ev = (self.dq[eng][slot], 16 * (n // NR + 1))
            op.sig = True
        out = []
        seen = set()
        for d in deps:
            if id(d) in seen or d is op or d.ph != self.phase_n:
                continue
            seen.add(id(d))
            if (not d.dma) and (not dma) and d.eng == eng:
                if eng == "pe" or not SAME_ENGINE_SYNC:
                    continue
            d.sig = True
            out.append(d)
        op.deps = out
        for b in r:
            if dma:
                b.rd.append(op)
            else:
                b.rd = [o for o in b.rd if o.dma or o.eng != eng] + [op]
        for b in w:
            b.lw = op
            b.rd = []
        self.ops[eng].append(op)
        return op

    def _emit_engine(self, e, eng):
        wt = self.waited[e]
        if self.phase_n > 0:
            eng.wait_ge(self.phase_sem, len(ENG) * self.phase_n)
        for op in self.ops[e]:
            need = {}
            for d in op.deps:
                sem, val = d.ev
                k = id(sem)
                if wt.get(k, 0) < val and need.get(k, (None, 0))[1] < val:
                    need[k] = (sem, val)
            for k, (sem, val) in need.items():
                eng.wait_ge(sem, val)
                wt[k] = val
            ins = op.fn(eng)
            if op.sig:
                ins.then_inc(op.ev[0], 16 if op.dma else 1)
        if e in DMAQ:
            for o in self.dlast[e]:
                if o is not None:
                    sem, val = o.ev
                    if wt.get(id(sem), 0) < val:
                        eng.wait_ge(sem, val)
                        wt[id(sem)] = val
        eng.drain().then_inc(self.phase_sem, 1)

    def emit(self):
        for e in ENG:
            c = self.ecnt[e]
            for op in self.ops[e]:
                if (not op.dma) and op.sig:
                    c += 1
                    op.ev = (self.esem[e], c)
            self.ecnt[e] = c
        with self.nc.Block() as block:
            for e in ENG:
                getattr(block, BLK[e])(lambda eng, e=e: self._emit_engine(e, eng))
        self.phase_n += 1
        self.ops = {e: [] for e in ENG}

    def mm(self, out, lhsT, rhs, start, stop, r=(), w=(), skip=False):
        return self.add("pe", lambda e: e.matmul(out, lhsT, rhs, start=start, stop=stop, skip_group_check=skip), r, w)

    def tr(self, out, in_, ident, r=(), w=()):
        return self.add("pe", lambda e: e.transpose(out, in_, ident), r, w)

    def act(self, out, in_, func, r=(), w=(), scale=None, accum_out=None, bias=None):
        kw = {}
        if scale is not None:
            kw["scale"] = scale
        if accum_out is not None:
            kw["accum_out"] = accum_out
        if bias is not None:
            kw["bias"] = bias
        return self.add("act", lambda e: e.activation(out=out, in_=in_, func=func, **kw), r, w)

    def tt(self, eng, out, in0, in1, op, r=(), w=()):
        return self.add(eng, lambda e: e.tensor_tensor(out=out, in0=in0, in1=in1, op=op), r, w)

    def ts(self, eng, out, in0, s1, s2, op0, op1, r=(), w=()):
        if op1 is None:
            return self.add(eng, lambda e: e.tensor_scalar(out=out, in0=in0, scalar1=s1, scalar2=None, op0=op0), r, w)
        return self.add(eng, lambda e: e.tensor_scalar(out=out, in0=in0, scalar1=s1, scalar2=s2, op0=op0, op1=op1), r, w)

    def stt(self, out, in0, scalar, in1, op0, op1, r=(), w=()):
        return self.add("dve", lambda e: e.scalar_tensor_tensor(out=out, in0=in0, scalar=scalar, in1=in1, op0=op0, op1=op1), r, w)

    def copy(self, eng, out, in_, r=(), w=()):
        if eng == "act":
            return self.add("act", lambda e: e.copy(out=out, in_=in_), r, w)
        return self.add(eng, lambda e: e.tensor_copy(out=out, in_=in_), r, w)

    def dma(self, q, out, in_, r=(), w=()):
        return self.add(q, lambda e: e.dma_start(out=out, in_=in_), r, w, dma=True)


class WStream:
    def __init__(self, S, ring, srcs, depth):
        self.S, self.ring, self.srcs, self.depth = S, ring, list(srcs), depth
        self.slots = []
        self.i = 0

    def prime(self):
        while len(self.slots) < min(len(self.srcs), self.i + self.depth):
            t, b = self.ring.next()
            self.S.dma("pool", t[:], self.srcs[len(self.slots)], w=[b])
            self.slots.append((t, b))

    def get(self):
        while len(self.slots) < min(len(self.srcs), self.i + self.depth):
            t, b = self.ring.next()
            self.S.dma("pool", t[:], self.srcs[len(self.slots)], w=[b])
            self.slots.append((t, b))
        r = self.slots[self.i]
        self.i += 1
        return r


class Ring:
    def __init__(self, items):
        self.items = [(t, Buf()) for t in items]
        self.i = 0

    def next(self):
        it = self.items[self.i % len(self.items)]
        self.i += 1
        return it


def build_nc(debug=False):
    nc = bass.Bass("TRN2", target_bir_lowering=False)
    dt = nc.dram_tensor
    x = dt("x", [2048, 2048], F32, kind="ExternalInput").ap()
    mem = dt("mem", [256, 2048], F32, kind="ExternalInput").ap()
    w_in = dt("w_in", [136, 128, 2048], F32, kind="ExternalInput").ap()
    w_kv = dt("w_kv", [16, 128, 2048], F32, kind="ExternalInput").ap()
    w_pab = dt("w_pab", [16, 128, 1536], F32, kind="ExternalInput").ap()
    w_pm = dt("w_pm", [16, 128, 1024], F32, kind="ExternalInput").ap()
    w_out = dt("w_out", [16, 128, 2048], F32, kind="ExternalInput").ap()
    g_in = dt("g_in", [128, 2048], F32, kind="ExternalInput").ap()
    g_mem = dt("g_mem", [128, 2048], F32, kind="ExternalInput").ap()
    g_fin = dt("g_fin", [128, 2048], F32, kind="ExternalInput").ap()
    cos_d = dt("cosT", [128, 2048], F32, kind="ExternalInput").ap()
    sin_d = dt("sinX", [128, 2048], F32, kind="ExternalInput").ap()
    cmask_d = dt("cmask", [128, 2048], F32, kind="ExternalInput").ap()
    bmask_d = dt("bmask", [128, 1280], F32, kind="ExternalInput").ap()
    ident_d = dt("ident", [128, 128], F32, kind="ExternalInput").ap()
    negsel_d = dt("negsel", [128, 1024], F32, kind="ExternalInput").ap()
    pastb_d = dt("pastb", [128, 64], F32, kind="ExternalInput").ap()
    pastm_d = dt("pastm", [128, 64], F32, kind="ExternalInput").ap()
    out = dt("out", [2048, 2048], F32, kind="ExternalOutput").ap()
    skind = "ExternalOutput" if debug else "Internal"
    sg = dt("sg_scr", [16, 4, 128, 3, 512], BF16, kind=skind).ap()
    yT = dt("yT_scr", [20, 128, 2048], BF16, kind=skind).ap()
    wo_scr = dt("wo_scr", [16, 128, 2048], BF16, kind="Internal").ap()

    with ExitStack() as es:
        S = Sched(nc, es)
        sb = lambda st, name, shape, dtype: st.enter_context(nc.sbuf_tensor("sb_" + name, shape, dtype))
        ps = lambda st, name, shape, dtype: st.enter_context(nc.psum_tensor("ps_" + name, shape, dtype))
        ident = sb(es, "ident", [128, 128], BF16)
        ones = sb(es, "ones", [128, 128], BF16)
        identb = Buf()

        es_h = ExitStack()
        hT = sb(es_h, "hT", [128, 16, 2048], BF16)
        memT = sb(es_h, "memT", [128, 16, 256], BF16)

        def proj_block(wtile, wbuf, rhs_fn, nqb, N, ring, epi, hook=None):
            for qb in range(nqb):
                pt_, pb_ = ring.next()
                for kc in range(16):
                    S.mm(pt_[:, 0:N], wtile[:, kc * 128:(kc + 1) * 128], rhs_fn(kc, qb), kc == 0, kc == 15, r=[wbuf], w=[pb_])
                epi(qb, pt_, pb_)
                if qb == 0 and hook is not None:
                    hook()

        h_rhs = lambda kc, qb: hT[:, kc, qb * 512:(qb + 1) * 512]

        with ExitStack() as pe_:
            xt = Ring([sb(pe_, "xt%d" % i, [128, 2048], F32) for i in range(4)])
            hb = Ring([sb(pe_, "hb%d" % i, [128, 2048], BF16) for i in range(3)])
            junk = sb(pe_, "junk", [128, 2048], BF16)
            grep = sb(pe_, "grep", [128, 2048], F32)
            gmrep = sb(pe_, "gmrep", [128, 2048], F32)
            st = sb(pe_, "st", [128, 4 * 18], F32)
            tp = Ring([ps(pe_, "tp%d" % i, [128, 512], BF16) for i in range(4)])
            wr = Ring([sb(pe_, "wr%d" % i, [128, 2048], BF16) for i in range(8)])
            sgt = Ring([sb(pe_, "sgt%d" % i, [128, 2048], BF16) for i in range(8)])
            pr = Ring([ps(pe_, "pr%d" % i, [128, 512], F32) for i in range(4)])
            gb, gmb, junkb = Buf(), Buf(), Buf()
            hTb = {(t_, g_): Buf() for t_ in range(16) for g_ in range(4)}
            S.dma("pool", ident[:], ident_d, w=[identb])
            S.add("dve", lambda e: e.memset(ones[:], 1.0))
            S.dma("sp", grep[:], g_in, w=[gb])
            S.dma("sp", gmrep[:], g_mem, w=[gmb])
            ws1 = WStream(S, wr, [w_in[88 + gi] for gi in range(48)], 2)
            cpc = [0]
            stt_ = {}

            def stA(tt_):
                is_mem = tt_ >= 16
                src = mem[(tt_ - 16) * 128:(tt_ - 15) * 128, :] if is_mem else x[tt_ * 128:(tt_ + 1) * 128, :]
                xtile, xb = xt.next()
                S.dma("sp", xtile[:], src, w=[xb])
                stb = Buf()
                c0 = tt_ * 4
                S.act(junk[:], xtile[:], AF.Square, r=[xb], w=[junkb, stb], accum_out=st[:, c0:c0 + 1])
                stt_[tt_] = (xtile, xb, stb, c0, is_mem)

            def stB(tt_):
                xtile, xb, stb, c0, is_mem = stt_[tt_]
                S.ts("dve", st[:, c0 + 1:c0 + 2], st[:, c0:c0 + 1], 1.0 / D, EPS, ALU.mult, ALU.add, r=[stb], w=[stb])
                S.act(st[:, c0 + 2:c0 + 3], st[:, c0 + 1:c0 + 2], AF.Sqrt, r=[stb], w=[stb])
                S.add("dve", lambda e, o=st[:, c0 + 3:c0 + 4], i=st[:, c0 + 2:c0 + 3]: e.reciprocal(out=o, in_=i), r=[stb], w=[stb])

            def stC(tt_):
                xtile, xb, stb, c0, is_mem = stt_[tt_]
                hbt, hbb = hb.next()
                S.stt(hbt[:], xtile[:], st[:, c0 + 3:c0 + 4], (gmrep if is_mem else grep)[:], ALU.mult, ALU.mult,
                      r=[xb, stb, gmb if is_mem else gb], w=[hbb])
                stt_[tt_] = (hbt, hbb, is_mem)

            def stD(tt_):
                hbt, hbb, is_mem = stt_[tt_]
                for g4 in range(4):
                    tpt, tpb = tp.next()
                    for k4 in range(4):
                        kc = g4 * 4 + k4
                        S.tr(tpt[:, k4 * 128:(k4 + 1) * 128], hbt[:, kc * 128:(kc + 1) * 128], ident[:], r=[hbb, identb], w=[tpb])
                    if is_mem:
                        dst = memT[:, g4 * 4:g4 * 4 + 4, (tt_ - 16) * 128:(tt_ - 15) * 128]
                        wbufs = []
                    else:
                        dst = hT[:, g4 * 4:g4 * 4 + 4, tt_ * 128:(tt_ + 1) * 128]
                        wbufs = [hTb[(tt_, g4)]]
                    S.copy("act" if cpc[0] % 2 == 0 else "dve", dst, tpt[:].rearrange("p (a b) -> p a b", a=4), r=[tpb], w=wbufs)
                    cpc[0] += 1

            blk = {}

            def gate_bank(gi, qb):
                if gi not in blk:
                    wt_, wb_ = ws1.get()
                    sgtile, sgb = sgt.next()
                    blk[gi] = (wt_, wb_, sgtile, sgb)
                wt_, wb_, sgtile, sgb = blk[gi]
                pt_, pb_ = pr.next()
                for kc in range(16):
                    S.mm(pt_[:], wt_[:, kc * 128:(kc + 1) * 128], hT[:, kc, qb * 512:(qb + 1) * 512], kc == 0, kc == 15,
                         r=[wb_] + [hTb[(4 * qb + i_, kc // 4)] for i_ in range(4)], w=[pb_])
                S.act(sgtile[:, qb * 512:(qb + 1) * 512], pt_[:], AF.Sigmoid, r=[pb_], w=[sgb])
                if qb == 3:
                    S.dma("sp", sg[gi % 16, :, :, gi // 16, :].rearrange("q p f -> p q f"), sgtile[:].rearrange("p (q f) -> p q f", q=4), r=[sgb])

            G = 6
            pending = [(gi, qb) for qb in range(4) for gi in range(G)]
            for step in range(18 + 3):
                if step < 18:
                    stA(step)
                if 0 <= step - 1 < 18:
                    stB(step - 1)
                if 0 <= step - 2 < 18:
                    stC(step - 2)
                if 0 <= step - 3 < 18:
                    stD(step - 3)
                for _ in range(3):
                    if pending and 4 * pending[0][1] + 3 <= step - 3:
                        gate_bank(*pending.pop(0))
            while pending:
                gate_bank(*pending.pop(0))
            ws1.depth = 6
            for gi in range(G, 48):
                if 8 <= gi < 24:
                    S.dma("pool", wo_scr[gi - 8], w_out[gi - 8])
                for qb in range(4):
                    gate_bank(gi, qb)
            S.emit()

        def perm_ap(t2d, qb, dl):
            if dl == 1:
                return t2d[:, qb * 512:(qb + 1) * 512]
            n_i = 512 // dl
            v = t2d.rearrange("p (r i) -> p r i", r=dl)[:, :, qb * n_i:(qb + 1) * n_i]
            return v.rearrange("p r i -> p i r")

        def nat_ap(t2d, dl):
            if dl == 1:
                return t2d
            return t2d.rearrange("p (i r) -> p i r", r=dl)

        with ExitStack() as pa_:
            cosT = sb(pa_, "cosT", [128, 2048], F32)
            sinX = sb(pa_, "sinX", [128, 2048], F32)
            bmask = sb(pa_, "bmask", [128, 1280], BF16)
            wr = Ring([sb(pa_, "wrA%d" % i, [128, 2048], BF16) for i in range(4)])
            ra = Ring([sb(pa_, "ra%d" % i, [128, 512], F32) for i in range(2)])
            rb = Ring([sb(pa_, "rb%d" % i, [128, 512], F32) for i in range(2)])
            pring = Ring([sb(pa_, "pt%d" % i, [128, 512], BF16) for i in range(4)])
            rl = Ring([sb(pa_, "rl%d" % i, [128, 512], F32) for i in range(2)])
            sgr = Ring([sb(pa_, "sgr%d" % i, [128, 512], F32) for i in range(2)])
            o32 = Ring([sb(pa_, "o32%d" % i, [128, 512], F32) for i in range(2)])
            ystage = Ring([sb(pa_, "ys%d" % i, [128, 2048], BF16) for i in range(2)])
            vTb = Buf()
            vTh = [None]
            constb = Buf()
            first_consts = [True]

            def load_consts():
                S.dma("sp", cosT[:], cos_d, w=[constb])
                S.dma("sp", sinX[:], sin_d, w=[constb])
                S.dma("pool", bmask[:], bmask_d, w=[constb])

            def rope_epi(dest, destb, dl):
                def epi(qb, pt_, pb_):
                    a, ab = ra.next()
                    b, bb = rb.next()
                    sl_ = slice(qb * 512, (qb + 1) * 512)
                    S.tt("dve", a[:], pt_[:], cosT[:, sl_], ALU.mult, r=[pb_, constb], w=[ab])
                    S.tt("dve", b[0:64, :], pt_[64:128, :], sinX[64:128, sl_], ALU.mult, r=[pb_, constb], w=[bb])
                    S.tt("dve", b[64:128, :], pt_[0:64, :], sinX[0:64, sl_], ALU.mult, r=[pb_, constb], w=[bb])
                    S.tt("pool", perm_ap(dest[:], qb, dl), nat_ap(a[:], dl), nat_ap(b[:], dl), ALU.add, r=[ab, bb], w=[destb])
                return epi

            def copy_epi(dest, destb, dl):
                def epi(qb, pt_, pb_):
                    S.copy("act", perm_ap(dest[:], qb, dl), nat_ap(pt_[:], dl), r=[pb_], w=[destb])
                return epi

            def silu_epi(dest, destb):
                def epi(qb, pt_, pb_):
                    t_, tb_ = sgr.next()
                    S.act(t_[:], pt_[:], AF.Exp, r=[pb_], w=[tb_], scale=-1.0)
                    S.act(t_[:], t_[:], AF.Ln, r=[tb_], w=[tb_], bias=1.0)
                    S.act(t_[:], t_[:], AF.Exp, r=[tb_], w=[tb_], scale=-1.0)
                    S.tt("dve", dest[:, qb * 512:(qb + 1) * 512], pt_[:], t_[:], ALU.mult, r=[pb_, tb_], w=[destb])
                return epi

            wsh = [None]
            srcs3 = []
            for j_ in range(4):
                for g_ in range(3):
                    srcs3 += [w_in[56 + g_ * 4 + j_], w_in[32 + g_ * 4 + j_], w_in[44 + g_ * 4 + j_]]
                srcs3.append(w_in[68 + j_])
            srcs4 = []
            for hm_ in range(4):
                for dc_ in range(2):
                    for kind_ in range(2):
                        srcs4.append(w_in[(72 if kind_ == 0 else 80) + hm_ * 2 + dc_])
                        if hm_ == 0:
                            b0_ = (dc_ * 2 + kind_) * 4
                            srcs4 += [w_kv[b_] for b_ in range(b0_, b0_ + 4)]
            ws3 = WStream(S, wr, srcs3, 3)
            ws4 = WStream(S, wr, srcs4, 3)

            def load_w(src):
                return wsh[0].get()

            def v_transposes(vtok, vtokb, vps, ntile=16):
                for g4 in range(ntile // 4):
                    vp, vpb = vps.next()
                    for k4 in range(4):
                        m = g4 * 4 + k4
                        S.tr(vp[:, k4 * 128:(k4 + 1) * 128], vTh[0][:, m * 128:(m + 1) * 128], ident[:], r=[vTb], w=[vpb])
                    S.copy("dve", vtok[:, g4 * 4:g4 * 4 + 4, :], vp[:].rearrange("p (a b) -> p a b", a=4), r=[vpb], w=[vtokb])

            def finish(Ob, Obb, Lb, Lbb, zT, zTb, qb, ys, ysb, zsl=None):
                r_, rlb = rl.next()
                o_, ob_ = o32.next()
                S.act(r_[:], Lb[:], AF.Ln, r=[Lbb], w=[rlb])
                S.act(r_[:], r_[:], AF.Exp, r=[rlb], w=[rlb], scale=-1.0)
                S.tt("dve", o_[:], Ob[:], r_[:], ALU.mult, r=[Obb, rlb], w=[ob_])
                zs = zT[:, qb * 512:(qb + 1) * 512] if zsl is None else zsl
                S.tt("pool", ys[:, qb * 512:(qb + 1) * 512], o_[:], zs, ALU.mult, r=[ob_, zTb], w=[ysb])

            with ExitStack() as pe_:
                sets = []
                for i in range(2):
                    sets.append(dict(
                        qT=sb(pe_, "qT%d" % i, [128, 2048], BF16), kT=sb(pe_, "kT%d" % i, [128, 2048], BF16),
                        vtok=sb(pe_, "vtok%d" % i, [128, 16, 128], BF16), zT=sb(pe_, "zT%d" % i, [128, 2048], F32),
                        nselT=sb(pe_, "nselT%d" % i, [128, 1024], BF16), kbar=sb(pe_, "kbar%d" % i, [128, 8], BF16),
                        qTb=Buf(), kTb=Buf(), vtokb=Buf(), zTb=Buf(), nselTb=Buf(), kbarb=Buf()))
                kb32 = sb(pe_, "kb32", [128, 8], F32)
                gm = sb(pe_, "gm", [128, 8, 8], F32)
                m8 = sb(pe_, "m8", [128, 8, 8], F32)
                ns1 = sb(pe_, "ns1", [128, 8, 8], F32)
                selbs = [Buf() for _ in range(8)]
                nsel = sb(pe_, "nsel", [128, 8, 128], F32)
                vT32 = sb(pe_, "vT32", [128, 2048], F32)
                ident32 = sb(pe_, "ident32", [128, 128], F32)
                vT32b = Buf()
                pr = Ring([ps(pe_, "prA%d" % i, [128, 512], F32) for i in range(2)])
                sring = Ring([ps(pe_, "sA%d" % i, [128, 512], F32) for i in range(3)])
                Obank = ps(pe_, "OA", [128, 512], F32)
                Lbank = ps(pe_, "LA", [128, 512], F32)
                gps = ps(pe_, "gpsA", [128, 64], F32)
                gpsb, nselb, Obb, Lbb, selb = [Buf() for _ in range(5)]
                cmask = sb(pe_, "cmask", [128, 128], BF16)
                negsel = sb(pe_, "negsel", [128, 1024], BF16)
                pastb = sb(pe_, "pastb", [128, 64], F32)
                pastm = sb(pe_, "pastm", [128, 64], F32)
                wsh[0] = WStream(S, wr, [w_in[b_] for h in range(8) for b_ in (16 + h, 24 + h, h, 8 + h)], 3)
                wsh[0].prime()
                load_consts()
                S.dma("pool", cmask[:], cmask_d[:, 0:128], w=[constb])
                S.dma("pool", negsel[:], negsel_d, w=[constb])
                S.dma("sp", pastb[:], pastb_d, w=[constb])
                S.dma("sp", pastm[:], pastm_d, w=[constb])
                S.dma("sp", ident32[:], ident_d, w=[constb])
                S.add("dve", lambda e: e.memset(nsel[:], 0.0), w=[nselb])
                scale = 128.0 ** -0.5

                def proj_gen(h, st):
                    for which in ("v", "z", "q", "k"):
                        wt_, wb_ = wsh[0].get()
                        if which == "q":
                            epi = rope_epi(st["qT"], st["qTb"], 1)
                        elif which == "k":
                            epi = rope_epi(st["kT"], st["kTb"], 1)
                        elif which == "v":
                            epi = copy_epi(vT32, vT32b, 1)
                        else:
                            epi = silu_epi(st["zT"], st["zTb"])
                        for qb in range(4):
                            pt_, pb_ = pr.next()
                            for kc in range(16):
                                S.mm(pt_[:], wt_[:, kc * 128:(kc + 1) * 128], hT[:, kc, qb * 512:(qb + 1) * 512], kc == 0, kc == 15,
                                     r=[wb_], w=[pb_])
                            epi(qb, pt_, pb_)
                            yield
                        if which == "z":
                            for g4 in range(4):
                                vp, vpb = pr.next()
                                for k4 in range(4):
                                    m = g4 * 4 + k4
                                    S.tr(vp[:, k4 * 128:(k4 + 1) * 128], vT32[:, m * 128:(m + 1) * 128], ident32[:], r=[vT32b, constb], w=[vpb])
                                S.copy("dve", st["vtok"][:, g4 * 4:g4 * 4 + 4, :], vp[:].rearrange("p (a b) -> p a b", a=4), r=[vpb], w=[st["vtokb"]])
                            yield
                def sel_gen(h, st):
                    kT, qT, kbar = st["kT"], st["qT"], st["kbar"]
                    S.add("dve", lambda e: e.tensor_reduce(out=kb32[:], in_=kT[:].rearrange("p (j t) -> p j t", j=8), axis=AX.X, op=ALU.add),
                          r=[st["kTb"]], w=[st["kbarb"]])
                    S.ts("dve", kbar[:], kb32[:], 1.0 / 256.0, None, ALU.mult, None, r=[st["kbarb"]], w=[st["kbarb"]])
                    yield
                    yield
                    for t8 in range(8):
                        S.mm(gps[:, t8 * 8:(t8 + 1) * 8], qT[:, (8 + t8) * 128:(9 + t8) * 128], kbar[:], True, True,
                             r=[st["qTb"], st["kbarb"]], w=[gpsb])
                    yield
                    for tp_ in range(0, 8, 4):
                        grp = list(range(tp_, tp_ + 4))
                        for t8 in grp:
                            jb = (8 + t8) // 2
                            S.tt("dve", gm[:, t8, :], gps[:, t8 * 8:(t8 + 1) * 8], pastb[:, jb * 8:(jb + 1) * 8], ALU.add,
                                 r=[gpsb, constb], w=[selbs[t8]])
                        for t8 in grp:
                            S.add("dve", lambda e, t8=t8: e.max(out=m8[:, t8, :], in_=gm[:, t8, :]), r=[selbs[t8]], w=[selbs[t8]])
                        for t8 in grp:
                            S.ts("dve", ns1[:, t8, :], gm[:, t8, :], m8[:, t8, 2:3], None, ALU.is_lt, None, r=[selbs[t8]], w=[selbs[t8]])
                        for t8 in grp:
                            jb = (8 + t8) // 2
                            S.tt("dve", nsel[:, t8, 0:8], ns1[:, t8, :], pastm[:, jb * 8:(jb + 1) * 8], ALU.mult,
                                 r=[selbs[t8], constb], w=[nselb])
                        yield
                    for g2_ in range(2):
                        vp, vpb = pr.next()
                        for k4 in range(4):
                            t8 = g2_ * 4 + k4
                            S.tr(vp[:, k4 * 128:(k4 + 1) * 128], nsel[:, t8, :], ident32[:], r=[nselb, constb], w=[vpb])
                        S.copy("act", st["nselT"][:, g2_ * 512:(g2_ + 1) * 512], vp[:], r=[vpb], w=[st["nselTb"]])
                        yield

                def attn_gen(h, st):
                    qT, kT, vtok, nselT = st["qT"], st["kT"], st["vtok"], st["nselT"]
                    ys, ysb = ystage.next()
                    tiles = [(qb, kt) for qb in range(4) for kt in range(4 * qb + 4)]

                    def s_tile(qb, kt):
                        sp_, sb_ = sring.next()
                        m_sel = qb >= 2 and kt <= 4 * qb + 1
                        m_c = kt >= 4 * qb
                        c0 = (kt - 4 * qb) * 128 if m_c else 0
                        S.mm(sp_[:, c0:512], kT[:, kt * 128:(kt + 1) * 128], qT[:, qb * 512 + c0:(qb + 1) * 512], True, not (m_sel or m_c),
                             r=[st["kTb"], st["qTb"]], w=[sb_], skip=True)
                        if m_sel:
                            j = kt // 2
                            S.mm(sp_[:, c0:512], negsel[:, j * 128:(j + 1) * 128], nselT[:, (qb - 2) * 512 + c0:(qb - 1) * 512], False, not m_c,
                                 r=[st["nselTb"], constb], w=[sb_], skip=True)
                        if m_c:
                            S.mm(sp_[:, c0:c0 + 128], ident[:], cmask[:, 0:128], False, True, r=[constb], w=[sb_], skip=True)
                        return sp_, sb_, c0
                    sts = {}
                    pts = {}

                    def do_exp(i):
                        sp_, sb_, c0 = sts.pop(i)
                        p_, pb_ = pring.next()
                        S.act(p_[:, c0:512], sp_[:, c0:512], AF.Exp, r=[sb_], w=[pb_], scale=scale)
                        pts[i] = (p_, pb_, c0)
                    n_t = len(tiles)
                    sts[0] = s_tile(*tiles[0])
                    sts[1] = s_tile(*tiles[1])
                    do_exp(0)
                    do_exp(1)
                    for i, (qb, kt) in enumerate(tiles):
                        if i + 2 < n_t:
                            sts[i + 2] = s_tile(*tiles[i + 2])
                            do_exp(i + 2)
                        nk = 4 * qb + 4
                        p_, pb_, c0 = pts.pop(i)
                        S.mm(Obank[:, c0:512], vtok[:, kt, :], p_[:, c0:512], kt == 0, kt == nk - 1, r=[pb_, st["vtokb"]], w=[Obb], skip=True)
                        S.mm(Lbank[:, c0:512], ones[:], p_[:, c0:512], kt == 0, kt == nk - 1, r=[pb_], w=[Lbb], skip=True)
                        yield "tile"
                        if kt == nk - 1:
                            finish(Obank, Obb, Lbank, Lbb, st["zT"], st["zTb"], qb, ys, ysb)
                            yield "fin"
                    S.dma("sp", yT[h], ys[:], r=[ysb])

                for _ in proj_gen(0, sets[0]):
                    pass
                for h in range(8):
                    if h == 7:
                        ws3.prime()
                    sg_ = sel_gen(h, sets[h % 2])
                    pg = proj_gen(h + 1, sets[(h + 1) % 2]) if h + 1 < 8 else None
                    k_ = 0
                    for tag in attn_gen(h, sets[h % 2]):
                        k_ += 1
                        if sg_ is not None:
                            if next(sg_, "done") == "done":
                                sg_ = None
                            if tag != "fin":
                                continue
                        if pg is not None and (tag == "fin" or k_ % 2 == 0):
                            if next(pg, "done") == "done":
                                pg = None
                    if pg is not None:
                        for _ in pg:
                            pass
                ws3.prime()
                S.emit()

            with ExitStack() as pe_:
                DL = (1, 4, 16)
                qTg = [sb(pe_, "qTg%d" % g, [128, 2048], BF16) for g in range(3)]
                kTg = [sb(pe_, "kTg%d" % g, [128, 2048], BF16) for g in range(3)]
                vtg = [sb(pe_, "vtg%d" % g, [128, 16, 128], BF16) for g in range(3)]
                zT = sb(pe_, "zTB", [128, 2048], F32)
                PT2 = sb(pe_, "PT2", [128, 2048], BF16)
                vT = sb(pe_, "vTB", [128, 2048], BF16)
                vTh[0] = vT
                pr = Ring([ps(pe_, "prB%d" % i, [128, 512], F32) for i in range(2)])
                sring = Ring([ps(pe_, "sB%d" % i, [128, 512], F32) for i in range(3)])
                Obank = ps(pe_, "OB", [128, 512], F32)
                Lbank = ps(pe_, "LB", [128, 512], F32)
                vps = Ring([ps(pe_, "vpsB", [128, 512], BF16)])
                qgb = [Buf() for _ in range(3)]
                kgb = [Buf() for _ in range(3)]
                vgb = [Buf() for _ in range(3)]
                zTb, PT2b, Obb, Lbb = Buf(), Buf(), Buf(), Buf()
                scale = 128.0 ** -0.5
                band = bmask[:, 0:256]
                band4c = bmask[:, 256:768]
                band4p = bmask[:, 768:1280]
                wsh[0] = ws3
                g0 = [dict(q=qTg[0], k=kTg[0], v=vtg[0], qb=qgb[0], kb=kgb[0], vb=vgb[0]),
                      dict(q=sb(pe_, "qTg0b", [128, 2048], BF16), k=sb(pe_, "kTg0b", [128, 2048], BF16),
                           v=sb(pe_, "vtg0b", [128, 16, 128], BF16), qb=Buf(), kb=Buf(), vb=Buf())]

                def bank_gen(wt_, wb_, epi, hook=None):
                    for qb in range(4):
                        pt_, pb_ = pr.next()
                        for kc in range(16):
                            S.mm(pt_[:], wt_[:, kc * 128:(kc + 1) * 128], hT[:, kc, qb * 512:(qb + 1) * 512], kc == 0, kc == 15,
                                 r=[wb_], w=[pb_])
                        epi(qb, pt_, pb_)
                        if qb == 0 and hook is not None:
                            hook()
                        yield

                def projB_gen(j, part):
                    for g in ((0,) if part == "g0" else (1, 2)):
                        if g == 0:
                            st0 = g0[j % 2]
                            qd, kd, vd, qdb, kdb, vdb = st0["q"], st0["k"], st0["v"], st0["qb"], st0["kb"], st0["vb"]
                        else:
                            qd, kd, vd, qdb, kdb, vdb = qTg[g], kTg[g], vtg[g], qgb[g], kgb[g], vgb[g]
                        wt_, wb_ = load_w(None)
                        yield from bank_gen(wt_, wb_, copy_epi(vT, vTb, DL[g]))
                        wt_, wb_ = load_w(None)
                        yield from bank_gen(wt_, wb_, rope_epi(qd, qdb, DL[g]), hook=lambda vd=vd, vdb=vdb: v_transposes(vd, vdb, vps))
                        wt_, wb_ = load_w(None)
                        yield from bank_gen(wt_, wb_, rope_epi(kd, kdb, DL[g]))
                    if part == "rest":
                        wt_, wb_ = load_w(None)
                        yield from bank_gen(wt_, wb_, silu_epi(zT, zTb))
                        for rb4 in range(4):
                            sp_, sb_ = sring.next()
                            for rr in range(4):
                                r = rb4 * 4 + rr
                                S.mm(sp_[:, rr * 128:(rr + 1) * 128], kTg[2][:, r * 128:(r + 1) * 128], qTg[2][:, r * 128:(r + 1) * 128],
                                     rr == 0, False, r=[kgb[2], qgb[2]], w=[sb_], skip=True)
                            S.mm(sp_[:], ident[:], band4c, False, True, r=[constb], w=[sb_], skip=True)
                            S.act(PT2[:, rb4 * 512:(rb4 + 1) * 512], sp_[:], AF.Exp, r=[sb_], w=[PT2b], scale=scale)
                        yield

                def attnB_gen(j):
                    st0 = g0[j % 2]
                    q0, k0, v0, q0b, k0b, v0b = st0["q"], st0["k"], st0["v"], st0["qb"], st0["kb"], st0["vb"]
                    ys, ysb = ystage.next()
                    for qb in range(4):
                        first = [True]

                        def pv(ocols_fn, vt, vtb, pap, rbufs):
                            st_ = first[0]
                            first[0] = False
                            S.mm(ocols_fn(Obank), vt, pap, st_, False, r=rbufs + [vtb], w=[Obb], skip=True)
                            S.mm(ocols_fn(Lbank), ones[:], pap, st_, False, r=rbufs, w=[Lbb], skip=True)
                        tiles = []
                        for kt in range(max(0, 4 * qb - 1), 4 * qb + 4):
                            qlo = max(kt, 4 * qb)
                            qhi = min(kt + 1, 4 * qb + 3)
                            ncol = (qhi - qlo + 1) * 128
                            ms = 0 if qlo == kt else 128
                            oc0 = (qlo - 4 * qb) * 128

                            def fs(sp_, sb_, kt=kt, qlo=qlo, qhi=qhi, ncol=ncol, ms=ms):
                                S.mm(sp_[:, 0:ncol], k0[:, kt * 128:(kt + 1) * 128], q0[:, qlo * 128:(qhi + 1) * 128], True, False,
                                     r=[k0b, q0b], w=[sb_])
                                S.mm(sp_[:, 0:ncol], ident[:], band[:, ms:ms + ncol], False, True, r=[constb], w=[sb_])

                            def fp(p_, pb_, kt=kt, ncol=ncol, oc0=oc0):
                                pv(lambda bk: bk[:, oc0:oc0 + ncol], v0[:, kt, :], v0b, p_[:, 0:ncol], [pb_])
                            tiles.append((fs, ncol, fp))
                        for kt in (qb - 1, qb):
                            if kt < 0:
                                continue

                            def fs(sp_, sb_, kt=kt, qb=qb):
                                for r in range(4):
                                    S.mm(sp_[:, r * 128:(r + 1) * 128], kTg[1][:, r * 512 + kt * 128:r * 512 + (kt + 1) * 128],
                                         qTg[1][:, r * 512 + qb * 128:r * 512 + (qb + 1) * 128], r == 0, False,
                                         r=[kgb[1], qgb[1]], w=[sb_], skip=True)
                                S.mm(sp_[:], ident[:], band4c if kt == qb else band4p, False, True, r=[constb], w=[sb_], skip=True)

                            def fp(p_, pb_, kt=kt):
                                for r in range(4):
                                    pv(lambda bk, r=r: bk[:].rearrange("p (i r) -> p i r", r=4)[:, :, r], vtg[1][:, r * 4 + kt, :], vgb[1],
                                       p_[:, r * 128:(r + 1) * 128], [pb_])
                            tiles.append((fs, 512, fp))
                        banks = {}

                        def issue(i):
                            sp_, sb_ = sring.next()
                            tiles[i][0](sp_, sb_)
                            banks[i] = (sp_, sb_)
                        pts = {}

                        def do_exp(i):
                            sp_, sb_ = banks.pop(i)
                            ncol = tiles[i][1]
                            p_, pb_ = pring.next()
                            S.act(p_[:, 0:ncol], sp_[:, 0:ncol], AF.Exp, r=[sb_], w=[pb_], scale=scale)
                            pts[i] = (p_, pb_)
                        issue(0)
                        do_exp(0)
                        if len(tiles) > 1:
                            issue(1)
                            do_exp(1)
                        for i in range(len(tiles)):
                            if i + 2 < len(tiles):
                                issue(i + 2)
                                do_exp(i + 2)
                            p_, pb_ = pts.pop(i)
                            tiles[i][2](p_, pb_)
                            yield "tile"
                        for r in range(16):
                            pv(lambda bk, r=r: bk[:].rearrange("p (i r) -> p i r", r=16)[:, :, r], vtg[2][:, r, :], vgb[2],
                               PT2[:, r * 128 + 32 * qb:r * 128 + 32 * qb + 32], [PT2b])
                        finish(Obank, Obb, Lbank, Lbb, zT, zTb, qb, ys, ysb)
                        yield "fin"
                    S.dma("sp", yT[8 + j], ys[:], r=[ysb])

                for _ in projB_gen(0, "g0"):
                    pass
                for _ in projB_gen(0, "rest"):
                    pass
                for j in range(4):
                    if j == 3:
                        ws4.prime()
                    pg = projB_gen(j + 1, "g0") if j + 1 < 4 else None
                    k_ = 0
                    for tag in attnB_gen(j):
                        k_ += 1
                        if pg is not None and (tag == "fin" or k_ % 2 == 0):
                            if next(pg, "done") == "done":
                                pg = None
                    if pg is not None:
                        for _ in pg:
                            pass
                    if j + 1 < 4:
                        for _ in projB_gen(j + 1, "rest"):
                            pass
                ws4.prime()
                S.emit()

            with ExitStack() as pe_:
                mkT = sb(pe_, "mkT", [128, 8, 256], BF16)
                vT = sb(pe_, "vTM", [128, 2048], BF16)
                vTh[0] = vT
                mvtok = sb(pe_, "mvtok", [128, 2, 1024], BF16)
                qmT = [sb(pe_, "qmT%d" % i, [128, 2048], BF16) for i in range(2)]
                zmT = [sb(pe_, "zmT%d" % i, [128, 2048], F32) for i in range(2)]
                pr = Ring([ps(pe_, "prM%d" % i, [128, 512], F32) for i in range(2)])
                sring = Ring([ps(pe_, "sM%d" % i, [128, 512], F32) for i in range(2)])
                Ob2 = [ps(pe_, "OM%d" % i, [128, 512], F32) for i in range(2)]
                Lbank = ps(pe_, "LM", [128, 512], F32)
                vps = Ring([ps(pe_, "vpsM", [128, 512], BF16)])
                mkb, mvb, Lbb = Buf(), Buf(), Buf()
                Obb2 = [Buf(), Buf()]
                qmb = [Buf(), Buf()]
                zmb = [Buf(), Buf()]
                m_rhs = lambda kc, qb: memT[:, kc, :]
                wsh[0] = ws4
                def kv_job(kb):
                    wt_, wb_ = load_w(None)
                    if kb < 8:
                        def epi(qb, pt_, pb_):
                            S.copy("act", mkT[:, kb, :], pt_[:, 0:256], r=[pb_], w=[mkb])
                        proj_block(wt_, wb_, m_rhs, 1, 256, pr, epi)
                    else:
                        blk = kb - 8

                        def epi(qb, pt_, pb_):
                            S.copy("act", vT[:, 0:256], pt_[:, 0:256], r=[pb_], w=[vTb])
                        proj_block(wt_, wb_, m_rhs, 1, 256, pr, epi)
                        vp, vpb = vps.next()
                        for mt in range(2):
                            S.tr(vp[:, mt * 128:(mt + 1) * 128], vT[:, mt * 128:(mt + 1) * 128], ident[:], r=[vTb], w=[vpb])
                        S.copy("dve", mvtok[:, :, blk * 128:(blk + 1) * 128], vp[:, 0:256].rearrange("p (a b) -> p a b", a=2), r=[vpb], w=[mvb])
                scale = 256.0 ** -0.5
                msets = []
                for i in range(2):
                    msets.append(dict(qm=qmT if i == 0 else [sb(pe_, "qmTb%d" % k, [128, 2048], BF16) for k in range(2)],
                                      zm=zmT if i == 0 else [sb(pe_, "zmTb%d" % k, [128, 2048], F32) for k in range(2)],
                                      qmb=[Buf(), Buf()], zmb=[Buf(), Buf()]))

                def projM_gen(hm, st):
                    for dc in range(2):
                        for kind in ("q", "z"):
                            wt_, wb_ = wsh[0].get()
                            epi = copy_epi(st["qm"][dc], st["qmb"][dc], 1) if kind == "q" else silu_epi(st["zm"][dc], st["zmb"][dc])
                            for qb in range(4):
                                pt_, pb_ = pr.next()
                                for kc in range(16):
                                    S.mm(pt_[:], wt_[:, kc * 128:(kc + 1) * 128], hT[:, kc, qb * 512:(qb + 1) * 512], kc == 0, kc == 15,
                                         r=[wb_], w=[pb_])
                                epi(qb, pt_, pb_)
                                yield

                def attnM_gen(hm, st):
                    ysl = [ystage.next(), ystage.next()]
                    for qb in range(4):
                        for mt in range(2):
                            sp_, sb_ = sring.next()
                            for dc in range(2):
                                S.mm(sp_[:], mkT[:, hm * 2 + dc, mt * 128:(mt + 1) * 128], st["qm"][dc][:, qb * 512:(qb + 1) * 512],
                                     dc == 0, dc == 1, r=[mkb, st["qmb"][dc]], w=[sb_])
                            p_, pb_ = pring.next()
                            S.act(p_[:], sp_[:], AF.Exp, r=[sb_], w=[pb_], scale=scale)
                            yield
                            for dvc in range(2):
                                S.mm(Ob2[dvc][:], mvtok[:, mt, hm * 256 + dvc * 128:hm * 256 + (dvc + 1) * 128], p_[:], mt == 0, mt == 1,
                                     r=[pb_, mvb], w=[Obb2[dvc]])
                            S.mm(Lbank[:], ones[:], p_[:], mt == 0, mt == 1, r=[pb_], w=[Lbb])
                        r_, rlb = rl.next()
                        S.act(r_[:], Lbank[:], AF.Ln, r=[Lbb], w=[rlb])
                        S.act(r_[:], r_[:], AF.Exp, r=[rlb], w=[rlb], scale=-1.0)
                        for dvc in range(2):
                            o_, ob_ = o32.next()
                            S.tt("dve", o_[:], Ob2[dvc][:], r_[:], ALU.mult, r=[Obb2[dvc], rlb], w=[ob_])
                            S.tt("pool", ysl[dvc][0][:, qb * 512:(qb + 1) * 512], o_[:], st["zm"][dvc][:, qb * 512:(qb + 1) * 512], ALU.mult,
                                 r=[ob_, st["zmb"][dvc]], w=[ysl[dvc][1]])
                        yield
                    for dvc in range(2):
                        S.dma("sp", yT[12 + hm * 2 + dvc], ysl[dvc][0][:], r=[ysl[dvc][1]])

                nb_ = 0
                for _ in projM_gen(0, msets[0]):
                    nb_ += 1
                    if nb_ % 4 == 0:
                        for kb in range(nb_ - 4, nb_):
                            kv_job(kb)
                for hm in range(4):
                    pg = projM_gen(hm + 1, msets[(hm + 1) % 2]) if hm + 1 < 4 else None
                    for _ in attnM_gen(hm, msets[hm % 2]):
                        if pg is not None:
                            if next(pg, "done") == "done":
                                pg = None
                    if pg is not None:
                        for _ in pg:
                            pass
                S.emit()

        es_h.close()

        with ExitStack() as pm_:
            mergedT = sb(pm_, "mergedT", [128, 16, 2048], BF16)
            with ExitStack() as pe_:
                ysb_ = sb(pe_, "yT_sb", [128, 20, 2048], BF16)
                wab = Ring([sb(pe_, "wab%d" % i, [128, 1536], BF16) for i in range(2)])
                wm = Ring([sb(pe_, "wm%d" % i, [128, 1024], BF16) for i in range(2)])
                gr = Ring([sb(pe_, "gr%d" % i, [128, 3, 512], BF16) for i in range(4)])
                t1r = Ring([sb(pe_, "t1_%d" % i, [128, 512], F32) for i in range(2)])
                t2r = Ring([sb(pe_, "t2_%d" % i, [128, 512], F32) for i in range(2)])
                t3r = Ring([sb(pe_, "t3_%d" % i, [128, 512], F32) for i in range(2)])
                s12 = Ring([sb(pe_, "s12_%d" % i, [128, 512], F32) for i in range(2)])
                PAr = Ring([ps(pe_, "PA%d" % i, [128, 512], F32) for i in range(2)])
                PBr = Ring([ps(pe_, "PB%d" % i, [128, 512], F32) for i in range(2)])
                PMr = Ring([ps(pe_, "PM%d" % i, [128, 512], F32) for i in range(2)])
                yb_ = [[Buf() for _ in range(4)] for _ in range(20)]
                for i in range(20):
                    S.dma("sp", ysb_[:, i, 0:512], yT[i][:, 0:512], w=[yb_[i][0]])
                def issue_w(c):
                    wa_, wab_b = wab.next()
                    wm_, wm_b = wm.next()
                    S.dma("pool", wa_[:], w_pab[c], w=[wab_b])
                    S.dma("pool", wm_[:], w_pm[c], w=[wm_b])
                    return wa_, wab_b, wm_, wm_b
                pend = [issue_w(0)]
                for qb in range(1, 4):
                    for i in range(20):
                        S.dma("pool", ysb_[:, i, qb * 512:(qb + 1) * 512], yT[i][:, qb * 512:(qb + 1) * 512], w=[yb_[i][qb]])
                for c in range(16):
                    if c + 1 < 16:
                        pend.append(issue_w(c + 1))
                    wa_, wab_b, wm_, wm_b = pend[c]
                    for qb in range(4):
                        g_, g_b = gr.next()
                        S.dma("sp", g_[:], sg[c, qb], w=[g_b])
                        qs = slice(qb * 512, (qb + 1) * 512)
                        pa, pab = PAr.next()
                        for ec in range(8):
                            S.mm(pa[:], wa_[:, ec * 128:(ec + 1) * 128], ysb_[:, ec, qs], ec == 0, ec == 7, r=[wab_b, yb_[ec][qb]], w=[pab])
                        pb, pbb = PBr.next()
                        for ec in range(4):
                            S.mm(pb[:], wa_[:, (8 + ec) * 128:(9 + ec) * 128], ysb_[:, 8 + ec, qs], ec == 0, ec == 3, r=[wab_b, yb_[8 + ec][qb]], w=[pbb])
                        pm, pmb = PMr.next()
                        for ec in range(8):
                            S.mm(pm[:], wm_[:, ec * 128:(ec + 1) * 128], ysb_[:, 12 + ec, qs], ec == 0, ec == 7, r=[wm_b, yb_[12 + ec][qb]], w=[pmb])
                        t1, t1b = t1r.next()
                        t2, t2b = t2r.next()
                        t3, t3b = t3r.next()
                        s_, s_b = s12.next()
                        S.tt("dve", t1[:], pa[:], g_[:, 0, :], ALU.mult, r=[pab, g_b], w=[t1b])
                        S.tt("dve", t2[:], pb[:], g_[:, 1, :], ALU.mult, r=[pbb, g_b], w=[t2b])
                        S.tt("dve", t3[:], pm[:], g_[:, 2, :], ALU.mult, r=[pmb, g_b], w=[t3b])
                        S.tt("pool", s_[:], t1[:], t2[:], ALU.add, r=[t1b, t2b], w=[s_b])
                        S.tt("pool", mergedT[:, c, qs], s_[:], t3[:], ALU.add, r=[s_b, t3b])
                S.emit()

            with ExitStack() as pe_:
                wo = sb(pe_, "wo", [128, 16, 2048], BF16)
                gf = sb(pe_, "gf", [128, 2048], F32)
                xt = Ring([sb(pe_, "xf%d" % i, [128, 2048], F32) for i in range(2)])
                rt = Ring([sb(pe_, "rf%d" % i, [128, 2048], F32) for i in range(2)])
                ot = Ring([sb(pe_, "of%d" % i, [128, 2048], F32) for i in range(2)])
                junk = sb(pe_, "junkf", [128, 2048], BF16)
                st = sb(pe_, "stf", [128, 128], F32)
                pr = Ring([ps(pe_, "pf%d" % i, [128, 512], F32) for i in range(8)])
                wob = [Buf() for _ in range(16)]
                gfb, junkb = Buf(), Buf()
                obcs = [[Buf() for _ in range(4)] for _ in range(2)]
                for dc in range(16):
                    S.dma("sp", wo[:, dc, :], wo_scr[dc], w=[wob[dc]])
                S.dma("sp", gf[:], g_fin, w=[gfb])
                for tt_ in range(16):
                    xtile, xb = xt.next()
                    S.dma("sp", xtile[:], x[tt_ * 128:(tt_ + 1) * 128, :], w=[xb])
                    r_, rb_ = rt.next()
                    stb = Buf()
                    c0 = tt_ * 8
                    rbs = [Buf() for _ in range(4)]
                    for cb in range(4):
                        p_, pb_ = pr.next()
                        cs = slice(cb * 512, (cb + 1) * 512)
                        for dc in range(16):
                            S.mm(p_[:], mergedT[:, dc, tt_ * 128:(tt_ + 1) * 128], wo[:, dc, cs], dc == 0, dc == 15,
                                 r=[wob[dc]], w=[pb_])
                        S.tt("dve", r_[:, cs], p_[:], xtile[:, cs], ALU.add, r=[pb_, xb], w=[rb_, rbs[cb]])
                        S.act(junk[:, cs], r_[:, cs], AF.Square, r=[rbs[cb]], w=[junkb, stb], accum_out=st[:, c0 + cb:c0 + cb + 1])
                    S.add("dve", lambda e, o=st[:, c0 + 4:c0 + 5], i=st[:, c0:c0 + 4]: e.tensor_reduce(out=o, in_=i, axis=AX.X, op=ALU.add),
                          r=[stb], w=[stb])
                    S.ts("dve", st[:, c0 + 5:c0 + 6], st[:, c0 + 4:c0 + 5], 1.0 / D, EPS, ALU.mult, ALU.add, r=[stb], w=[stb])
                    S.act(st[:, c0 + 6:c0 + 7], st[:, c0 + 5:c0 + 6], AF.Sqrt, r=[stb], w=[stb])
                    S.add("dve", lambda e, o=st[:, c0 + 7:c0 + 8], i=st[:, c0 + 6:c0 + 7]: e.reciprocal(out=o, in_=i), r=[stb], w=[stb])
                    o_, ob_ = ot.next()
                    for cb in range(4):
                        cs = slice(cb * 512, (cb + 1) * 512)
                        obc = obcs[tt_ % 2][cb]
                        S.stt(o_[:, cs], r_[:, cs], st[:, c0 + 7:c0 + 8], gf[:, cs], ALU.mult, ALU.mult, r=[rb_, stb, gfb], w=[obc])
                        S.dma("sp", out[tt_ * 128:(tt_ + 1) * 128, cs], o_[:, cs], r=[obc])
                S.emit()
    return nc


def _host_consts():
    half = 64
    inv = 10000.0 ** (-np.arange(half, dtype=np.float64) / float(half))
    ang = np.arange(2048, dtype=np.float64)[:, None] * inv[None, :]
    cos = np.cos(ang).astype(np.float32).T
    sin = np.sin(ang).astype(np.float32).T
    cosT = np.concatenate([cos, cos], axis=0)
    sinX = np.concatenate([sin, -sin], axis=0)
    kp = np.arange(128)[:, None]
    cm = []
    for o in range(4):
        qf = np.arange(512)[None, :]
        cm.append(np.where(o * 128 + kp <= qf, 0.0, NEG))
    cmask = np.concatenate(cm, axis=1).astype(np.float32)
    qf = np.arange(256)[None, :]
    dlt = qf - kp
    band = np.where((dlt >= 0) & (dlt <= 128), 0.0, NEG).astype(np.float32)
    bmask = np.concatenate([band, np.tile(band[:, 0:128], (1, 4)), np.tile(band[:, 128:256], (1, 4))], axis=1).astype(np.float32)
    ident = np.eye(128, dtype=np.float32)
    negsel = np.zeros((128, 8, 128), np.float32)
    for j in range(8):
        negsel[j, j, :] = NEG
    negsel = negsel.reshape(128, 1024)
    pastb = np.zeros((128, 8, 8), np.float32)
    pastm = np.zeros((128, 8, 8), np.float32)
    for jb in range(8):
        for j in range(8):
            pastb[:, jb, j] = 0.0 if j < jb else -1e30
            pastm[:, jb, j] = 1.0 if j < jb else 0.0
    return dict(cosT=np.ascontiguousarray(cosT), sinX=np.ascontiguousarray(sinX), cmask=cmask, bmask=bmask, ident=ident,
                negsel=negsel, pastb=pastb.reshape(128, 64), pastm=pastm.reshape(128, 64))


def _blk(w, nb):
    return np.ascontiguousarray(w.reshape(16, 128, nb, 128).transpose(2, 1, 0, 3).reshape(nb, 128, 2048))


def _host_layout(inputs):
    f = lambda a: np.ascontiguousarray(np.asarray(a, dtype=np.float32))
    w_in = _blk(f(inputs["w_in"])[0], 136)
    w_kv = _blk(f(inputs["w_mem_kv"])[0], 16)
    wa = f(inputs["w_proj_a"])[0].reshape(8, 128, 16, 128).transpose(2, 1, 0, 3)
    wb = f(inputs["w_proj_b"])[0].reshape(4, 128, 16, 128).transpose(2, 1, 0, 3)
    wm = f(inputs["w_proj_m"])[0].reshape(8, 128, 16, 128).transpose(2, 1, 0, 3)
    w_pab = np.ascontiguousarray(np.concatenate([wa, wb], axis=2).reshape(16, 128, 1536))
    w_pm = np.ascontiguousarray(wm.reshape(16, 128, 1024))
    w_out = np.ascontiguousarray(f(inputs["w_out"])[0].reshape(16, 128, 2048))
    rep = lambda g: np.ascontiguousarray(np.broadcast_to(f(g).reshape(1, 2048), (128, 2048)))
    shared = dict(w_in=w_in, w_kv=w_kv, w_pab=w_pab, w_pm=w_pm, w_out=w_out,
                  g_in=rep(inputs["norm_in_g"]), g_mem=rep(inputs["norm_mem_g"]), g_fin=rep(inputs["norm_final_g"]))
    shared.update(_host_consts())
    return shared


_NC_CACHE = {}


def kernel(x, mem, norm_in_g, norm_mem_g, w_in, w_mem_kv, w_proj_a, w_proj_b, w_proj_m, w_out, norm_final_g):
    inputs = dict(x=x, mem=mem, norm_in_g=norm_in_g, norm_mem_g=norm_mem_g, w_in=w_in, w_mem_kv=w_mem_kv,
                  w_proj_a=w_proj_a, w_proj_b=w_proj_b, w_proj_m=w_proj_m, w_out=w_out, norm_final_g=norm_final_g)
    shared = _host_layout(inputs)
    xs = np.asarray(x, dtype=np.float32)
    ms = np.asarray(mem, dtype=np.float32)
    n = 8
    nc = build_nc()
    in_maps = []
    for b in range(n):
        m = dict(shared)
        m["x"] = np.ascontiguousarray(xs[b])
        m["mem"] = np.ascontiguousarray(ms[b])
        in_maps.append(m)
    res = run_bass_kernel_spmd(nc, in_maps, core_ids=list(range(n)))
    return np.stack([np.asarray(r["out"], dtype=np.float32) for r in res.results], axis=0)
```
